# Optimizing a Trainium2 kernel written in Bass

```python
import jax, jax.numpy as jnp
from jax import lax
import numpy as np

D_MODEL = 2048
BATCH = 2
SEQ = 8192
DEPTH = 4

D_MIX = D_MODEL
RET_HEADS = 4
RET_HEAD_DIM = 256
D_RET = RET_HEADS * RET_HEAD_DIM
D_SC = D_MIX // 4
D_CF = D_MIX - D_RET - D_SC
SC_WIDTH = 3
CF_WIDTH = 31
RET_CHUNK = 128
ROPE_BASE = 10000.0
EPS = 1e-6
SPLIT_SIZES = [D_RET] * 4 + [D_SC] * 4 + [D_CF] * 3
D_IN = sum(SPLIT_SIZES)
SPLIT_POINTS = [int(s) for s in np.cumsum(SPLIT_SIZES)[:-1]]

kernel_name = "hybrid_retention_shortconv_conformer_adaln"


def rms_norm(x, g):
    xf = x.astype(jnp.float32)
    y = xf * lax.rsqrt(jnp.mean(xf * xf, axis=-1, keepdims=True) + EPS)
    return (y * g.astype(jnp.float32)).astype(x.dtype)


def layer_norm(x, g, b):
    xf = x.astype(jnp.float32)
    mu = jnp.mean(xf, axis=-1, keepdims=True)
    xc = xf - mu
    y = xc * lax.rsqrt(jnp.mean(xc * xc, axis=-1, keepdims=True) + EPS)
    return (y * g.astype(jnp.float32) + b.astype(jnp.float32)).astype(x.dtype)


def causal_dwconv(u, w):
    K, C = w.shape
    return lax.conv_general_dilated(
        u, w[:, None, :].astype(u.dtype), window_strides=(1,), padding=[(K - 1, 0)],
        dimension_numbers=("NWC", "WIO", "NWC"), feature_group_count=C)


def rotary(t, positions):
    d = t.shape[-1]
    inv_freq = ROPE_BASE ** (-jnp.arange(0, d // 2, dtype=jnp.float32) / (d // 2))
    ang = positions.astype(jnp.float32)[..., None] * inv_freq
    cos = jnp.cos(ang)[:, :, None, :]
    sin = jnp.sin(ang)[:, :, None, :]
    t1, t2 = t[..., : d // 2], t[..., d // 2:]
    return jnp.concatenate([t1 * cos - t2 * sin, t2 * cos + t1 * sin], axis=-1)


def retention_chunkwise(q, k, v):
    Bsz, T, H, dk = q.shape
    dv = v.shape[-1]
    n_chunks = T // RET_CHUNK
    log_gamma = jnp.log1p(-jnp.exp2(-5.0 - jnp.arange(H, dtype=jnp.float32)))
    idx = jnp.arange(RET_CHUNK, dtype=jnp.float32)
    diff = idx[:, None] - idx[None, :]
    intra = jnp.where(diff[None] >= 0,
                      jnp.exp(jnp.maximum(diff, 0.0)[None] * log_gamma[:, None, None]), 0.0)
    q_decay = jnp.exp((idx[None] + 1.0) * log_gamma[:, None])[None, :, :, None]
    k_decay = jnp.exp((RET_CHUNK - 1.0 - idx[None]) * log_gamma[:, None])[None, :, :, None]
    chunk_decay = jnp.exp(RET_CHUNK * log_gamma)[None, :, None, None]

    def to_chunks(t):
        return t.reshape(Bsz, n_chunks, RET_CHUNK, H, t.shape[-1]).transpose(1, 0, 3, 2, 4)

    def step(state, qkv):
        qc, kc, vc = qkv
        scores = jnp.einsum("bhid,bhjd->bhij", qc, kc) * intra[None]
        o_inner = jnp.einsum("bhij,bhjv->bhiv", scores, vc)
        o_cross = jnp.einsum("bhid,bhdv->bhiv", qc, state) * q_decay
        state = state * chunk_decay + jnp.einsum("bhjd,bhjv->bhdv", kc * k_decay, vc)
        return state, o_inner + o_cross

    state0 = jnp.zeros((Bsz, H, dk, dv), jnp.float32)
    _, out = lax.scan(step, state0, (to_chunks(q), to_chunks(k), to_chunks(v)))
    return out.transpose(1, 0, 3, 2, 4).reshape(Bsz, T, H, dv)


def group_norm_heads(o):
    mu = jnp.mean(o, axis=-1, keepdims=True)
    oc = o - mu
    return oc * lax.rsqrt(jnp.mean(oc * oc, axis=-1, keepdims=True) + EPS)


def hybrid_layer(x, c, positions, ada_w, ada_b, norm_g, w_in, sc_conv_w,
                 cf_conv_w, cf_conv_b, cf_ln_g, cf_ln_b, w_out):
    Bsz, T, _ = x.shape
    mod = jax.nn.silu(c) @ ada_w + ada_b
    shift, scale, gate = jnp.split(mod, 3, axis=-1)
    h = rms_norm(x, norm_g) * (1.0 + scale[:, None, :]) + shift[:, None, :]

    proj = h @ w_in
    (r_q, r_k, r_v, r_g, s_b, s_c, s_h, s_g, f_a, f_b, f_g) = jnp.split(proj, SPLIT_POINTS, axis=-1)

    hs = (Bsz, T, RET_HEADS, RET_HEAD_DIM)
    q = rotary(r_q.reshape(hs).astype(jnp.float32), positions)
    k = rotary(r_k.reshape(hs).astype(jnp.float32), positions) * (RET_HEAD_DIM ** -0.5)
    v = r_v.reshape(hs).astype(jnp.float32)
    ret = group_norm_heads(retention_chunkwise(q, k, v)).reshape(Bsz, T, D_RET).astype(x.dtype)
    ret = jax.nn.silu(r_g) * ret

    sc = s_b * causal_dwconv(s_c * s_h, sc_conv_w)
    sc = jax.nn.silu(s_g) * sc

    glu = f_a * jax.nn.sigmoid(f_b)
    cf = causal_dwconv(glu, cf_conv_w) + cf_conv_b
    cf = jax.nn.silu(layer_norm(cf, cf_ln_g, cf_ln_b))
    cf = jax.nn.silu(f_g) * cf

    mixed = jnp.concatenate([ret, sc, cf], axis=-1) @ w_out
    return x + gate[:, None, :] * mixed


def setup_inputs(seed: int = 0) -> dict:
    key = jax.random.key(seed)
    ks = jax.random.split(key, 14)
    f32 = jnp.float32
    x = jax.random.normal(ks[0], (BATCH, SEQ, D_MODEL), f32)
    c = jax.random.normal(ks[1], (BATCH, D_MODEL), f32)
    positions = jnp.broadcast_to(jnp.arange(SEQ, dtype=jnp.int32)[None, :], (BATCH, SEQ))
    ada_w = jax.random.normal(ks[2], (DEPTH, D_MODEL, 3 * D_MODEL), f32) * (0.5 * D_MODEL ** -0.5)
    ada_b = jax.random.normal(ks[3], (DEPTH, 3 * D_MODEL), f32) * 0.02
    norm_g = 1.0 + 0.02 * jax.random.normal(ks[4], (DEPTH, D_MODEL), f32)
    w_in = jax.random.normal(ks[5], (DEPTH, D_MODEL, D_IN), f32) * (D_MODEL ** -0.5)
    sc_conv_w = jax.random.normal(ks[6], (DEPTH, SC_WIDTH, D_SC), f32) * (SC_WIDTH ** -0.5)
    cf_conv_w = jax.random.normal(ks[7], (DEPTH, CF_WIDTH, D_CF), f32) * (CF_WIDTH ** -0.5)
    cf_conv_b = jax.random.normal(ks[8], (DEPTH, D_CF), f32) * 0.02
    cf_ln_g = 1.0 + 0.02 * jax.random.normal(ks[9], (DEPTH, D_CF), f32)
    cf_ln_b = jax.random.normal(ks[10], (DEPTH, D_CF), f32) * 0.02
    w_out = jax.random.normal(ks[11], (DEPTH, D_MIX, D_MODEL), f32) * (D_MIX ** -0.5)
    final_g = 1.0 + 0.02 * jax.random.normal(ks[12], (D_MODEL,), f32)
    return {"x": x, "c": c, "positions": positions, "ada_w": ada_w, "ada_b": ada_b,
            "norm_g": norm_g, "w_in": w_in, "sc_conv_w": sc_conv_w, "cf_conv_w": cf_conv_w,
            "cf_conv_b": cf_conv_b, "cf_ln_g": cf_ln_g, "cf_ln_b": cf_ln_b, "w_out": w_out,
            "final_g": final_g}


def reference(x, c, positions, ada_w, ada_b, norm_g, w_in, sc_conv_w, cf_conv_w,
              cf_conv_b, cf_ln_g, cf_ln_b, w_out, final_g):
    h = x
    for layer in range(DEPTH):
        h = hybrid_layer(h, c, positions, ada_w[layer], ada_b[layer], norm_g[layer], w_in[layer],
                         sc_conv_w[layer], cf_conv_w[layer], cf_conv_b[layer], cf_ln_g[layer],
                         cf_ln_b[layer], w_out[layer])
    return rms_norm(h, final_g)
```

```python
import math
import numpy as np
import concourse.bass as bass
import concourse.mybir as mybir
from concourse.bass_utils import run_bass_kernel_spmd

F32 = mybir.dt.float32
BF16 = mybir.dt.bfloat16
I32 = mybir.dt.int32
AF = mybir.ActivationFunctionType
ALU = mybir.AluOpType
AX = mybir.AxisListType

D = 2048
DIN = 7680
NH = 4
HD = 256
SCW = 3
CFW = 31
EPS = 1e-6
TT = 512
NS = 4
NB = 2
NTMP = 8
GAM = [1.0 - 2.0 ** (-5 - h) for h in range(NH)]
CDEC = [g ** 128 for g in GAM]
C1 = 6.28125
C2 = 2 * math.pi - 6.28125

OQ, OK_, OV, OG = 0, 1024, 2048, 3072
OSB, OSC, OSH, OSG = 4096, 4608, 5120, 5632
OFA, OFB, OFG = 6144, 6656, 7168


class TR:
    def __init__(self, nc, ndma=8):
        self.nc = nc
        self.eng = {"pe": nc.tensor, "act": nc.scalar, "dve": nc.vector, "pool": nc.gpsimd, "sp": nc.sync}
        self.sem = {k: nc.alloc_semaphore(name="s_" + k) for k in self.eng}
        self.cnt = {k: 0 for k in self.eng}
        self.seen = {k: {} for k in self.eng}
        self.dsem = [nc.alloc_semaphore(name=f"dq{i}") for i in range(2 * ndma)]
        self.dcnt = [0] * (2 * ndma)
        self.ndma = ndma
        self.dnext = {"hw": 0, "sw": 0}
        self.lastw = {}
        self.reads = {}
        self.same = {"pe": False, "act": True, "dve": True, "pool": True, "sp": True}

    def _wait(self, e, tok):
        if tok is None:
            return
        kind, key, val = tok
        if kind == "e" and key == e and not self.same[e]:
            return
        sk = (kind, key)
        if self.seen[e].get(sk, 0) >= val:
            return
        self.seen[e][sk] = val
        sem = self.sem[key] if kind == "e" else self.dsem[key]
        self.eng[e].wait_ge(sem, val)

    def deps(self, e, reads, writes):
        for b in reads:
            self._wait(e, self.lastw.get(b))
        for b in writes:
            self._wait(e, self.lastw.get(b))
            for sk, v in self.reads.get(b, {}).items():
                self._wait(e, (sk[0], sk[1], v))

    def commit(self, tok, reads, writes):
        sk = (tok[0], tok[1])
        for b in reads:
            d = self.reads.setdefault(b, {})
            if d.get(sk, 0) < tok[2]:
                d[sk] = tok[2]
        for b in writes:
            self.lastw[b] = tok
            self.reads[b] = {}

    def op(self, e, fn, reads=(), writes=(), inc=True):
        self.deps(e, reads, writes)
        ins = fn()
        if inc:
            self.cnt[e] += 1
            ins.then_inc(self.sem[e], 1)
            tok = ("e", e, self.cnt[e])
        else:
            tok = ("e", e, self.cnt[e] + 1)
        self.commit(tok, reads, writes)
        return tok

    def dma(self, e, out, in_, reads=(), writes=(), **kw):
        self.deps(e, reads, writes)
        cls = "sw" if e == "pool" else "hw"
        i = self.dnext[cls] + (self.ndma if cls == "sw" else 0)
        self.dnext[cls] = (self.dnext[cls] + 1) % self.ndma
        if self.dcnt[i] > 0:
            self._wait(e, ("d", i, self.dcnt[i]))
        self.eng[e].dma_start(out=out, in_=in_, **kw).then_inc(self.dsem[i], 16)
        self.dcnt[i] += 16
        tok = ("d", i, self.dcnt[i])
        self.commit(tok, reads, writes)
        return tok

    def wait_all(self, e):
        for k in self.eng:
            if self.cnt[k] > 0:
                self._wait(e, ("e", k, self.cnt[k]))
        for i in range(len(self.dsem)):
            if self.dcnt[i] > 0:
                self._wait(e, ("d", i, self.dcnt[i]))


class _Stop(Exception):
    pass


DEBUG_STOP = None


def _stage(n):
    if DEBUG_STOP is not None and n >= DEBUG_STOP:
        raise _Stop()


def build_nc(T, L):
    nc_holder = {}
    try:
        _build_nc(T, L, nc_holder)
    except _Stop:
        nc_holder["tr"].wait_all("sp")
    return nc_holder["nc"]


def _build_nc(T, L, nc_holder):
    NT = T // TT
    nc = bass.Bass("TRN2", target_bir_lowering=False)
    dt_in = lambda name, shape, dt=F32: nc.dram_tensor(name, list(shape), dt, kind="ExternalInput").ap()
    x_in = dt_in("x", [T, D])
    pos_in = dt_in("pos", [1, T], I32)
    ccol_in = dt_in("ccol", [128, 16])
    adaw_in = dt_in("ada_w", [L, D, 3 * D])
    adab_in = dt_in("ada_b", [L, 3 * D])
    normg_in = dt_in("normg", [128, L, 16])
    win_in = dt_in("w_in", [L, D, DIN])
    wout_in = dt_in("w_out", [L, D, D])
    scw_in = dt_in("scw", [128, L, 4, SCW])
    cfw_in = dt_in("cfw", [128, L, 4, CFW])
    cfp_in = dt_in("cfp", [128, L, 3, 4])
    fg_in = dt_in("final_g", [1, D])
    cst_in = dt_in("cst", [128, 4 * 128 + 4 * 128 + 4 + 1])
    out = nc.dram_tensor("out", [T, D], F32, kind="ExternalOutput").ap()
    x_s = nc.dram_tensor("x_s", [T, D], F32).ap()
    cos_d = nc.dram_tensor("cos_d", [128, T], F32).ap()
    sin_d = nc.dram_tensor("sin_d", [128, T], F32).ap()

    tr = TR(nc)
    nc_holder["nc"] = nc
    nc_holder["tr"] = tr
    sb = lambda name, shape, dt=F32: nc.alloc_sbuf_tensor("sb_" + name, list(shape), dt)
    wbuf = [sb(f"wbuf{i}", [128, 16, 512], BF16) for i in range(NB)]
    xin = [sb("xin0", [128, D])]
    hb = sb("hb", [128, D], BF16)
    hT = sb("hT", [128, 16, TT], BF16)
    gate_bc = sb("gate_bc", [128, D])
    cos_t = sb("cos_t", [128, TT])
    sin_t = sb("sin_t", [128, TT])
    qT = sb("qT", [128, 8, TT], BF16)
    kT = sb("kT", [128, 8, TT], BF16)
    k_tm = sb("k_tm", [128, NS, 1024], BF16)
    v_tm = sb("v_tm", [128, NS, 1024], BF16)
    sg = sb("sg", [128, NS, 1024], BF16)
    S = sb("S", [128, 8, 256])
    Sbf = sb("Sbf", [128, 8, 256], BF16)
    mixedT = sb("mixedT", [128, 16, TT], BF16)
    tmp = [sb(f"tmp{i}", [128, TT]) for i in range(NTMP)]
    sT = sb("sT", [128, 4, 128], BF16)
    on = [sb(f"on{i}", [128, 256]) for i in range(2)]
    og = sb("og", [128, 1024], BF16)
    st6 = sb("st6", [128, 4, 6])
    mv = sb("mv", [128, 4, 2])
    gn = sb("gn", [128, 4, 3])
    ss = sb("ss", [128, 4])
    scy = sb("scy", [128, 4, TT])
    u_ext = sb("u_ext", [128, 4, TT + 2])
    cfy = sb("cfy", [128, 4, TT])
    glu_ext = sb("glu_ext", [128, 4, TT + 30])
    mean_sb = sb("mean_sb", [128, TT])
    rstd_sb = sb("rstd_sb", [128, TT])
    ident = sb("ident", [128, 128], BF16)
    identf = sb("identf", [128, 128])
    ones_b = sb("ones_b", [128, 128], BF16)
    cst = sb("cst", [128, 4 * 128 + 4 * 128 + 4 + 1])
    halfpi = sb("halfpi", [128, 1])
    epsb = sb("epsb", [128, 1])
    normg = sb("normg", [128, L, 16])
    scw = sb("scw", [128, L, 4, SCW])
    cfw = sb("cfw", [128, L, 4, CFW])
    cfp = sb("cfp", [128, L, 3, 4])
    ccol = sb("ccol", [128, 16])
    csil = sb("csil", [128, 16])
    csil_hi = sb("csil_hi", [128, 16], BF16)
    csil_lo = sb("csil_lo", [128, 16], BF16)
    gs = sb("gs", [128, 16])
    sclb = sb("sclb", [128, 16])
    shiftc = sb("shiftc", [128, 16])
    posi = sb("posi", [128, TT], I32)
    maskT = cst[:, 0:512].rearrange("p (h i) -> p h i", h=4)
    qdec = cst[:, 512:1024].rearrange("p (h i) -> p h i", h=4)
    kdec = cst[:, 1024:1028]
    invf = cst[:, 1028:1029]
    NMM = 3
    mm = [nc.alloc_psum_tensor(f"ps_mm{i}", [128, 512], F32) for i in range(NMM)]
    tp = [nc.alloc_psum_tensor(f"ps_tp{i}", [128, 1024], BF16) for i in range(2)]
    sc_ps = nc.alloc_psum_tensor("ps_sc", [128, 512], F32)
    o_pss = [nc.alloc_psum_tensor(f"ps_o{i}", [128, 512], F32) for i in range(2)]

    import os as _os
    V, A, P, PE = nc.vector, nc.scalar, nc.gpsimd, nc.tensor
    cnt = {"mm": 0, "tp": 0, "tmp": 0, "w": 0, "alt": 0}

    def nxt(k, n):
        i = cnt[k] % n
        cnt[k] += 1
        return i

    def dve(fn, r=(), w=()):
        return tr.op("dve", fn, r, w)

    def act(fn, r=(), w=()):
        return tr.op("act", fn, r, w)

    def pool(fn, r=(), w=()):
        return tr.op("pool", fn, r, w)

    def pe(fn, r=(), w=(), inc=True):
        if _os.environ.get("ALLINC"):
            inc = True
        return tr.op("pe", fn, r, w, inc=inc)

    def gettmp():
        i = nxt("tmp", NTMP)
        return tmp[i], f"tmp{i}"

    tr.dma("sp", cst[:], cst_in, writes=["cst"])
    tr.dma("sp", normg[:], normg_in, writes=["normg"])
    tr.dma("sp", scw[:], scw_in, writes=["scw"])
    tr.dma("sp", cfw[:], cfw_in, writes=["cfw"])
    tr.dma("sp", cfp[:], cfp_in, writes=["cfp"])
    tr.dma("sp", ccol[:], ccol_in, writes=["ccol"])
    pool(lambda: P.memset(identf[:], 0.0), w=["identf"])
    pool(lambda: P.affine_select(out=identf[:], in_=identf[:], pattern=[[-1, 128]], compare_op=ALU.not_equal,
                                 fill=1.0, base=0, channel_multiplier=1), r=["identf"], w=["identf"])
    dve(lambda: V.tensor_copy(out=ident[:], in_=identf[:]), r=["identf"], w=["ident"])
    dve(lambda: V.memset(ones_b[:], 1.0 / 512.0), w=["ones_b"])
    dve(lambda: V.memset(halfpi[:], math.pi / 2), w=["halfpi"])
    dve(lambda: V.memset(epsb[:], EPS), w=["epsb"])

    _stage(0)
    for t in range(NT):
        tsl = slice(t * TT, (t + 1) * TT)
        tr.dma("sp", posi[:], pos_in[:, tsl].partition_broadcast(128), writes=["posi"])
        ang, ka = gettmp()
        kf, kk = gettmp()
        dve(lambda: V.tensor_copy(out=kf[:], in_=posi[:]), r=["posi"], w=[kk])
        dve(lambda: V.tensor_scalar(out=ang[:], in0=kf[:], scalar1=invf, scalar2=None, op0=ALU.mult), r=[kk, "cst"], w=[ka])
        dve(lambda: V.tensor_scalar(out=kf[:], in0=ang[:], scalar1=1.0 / (2 * math.pi), scalar2=None, op0=ALU.mult), r=[ka], w=[kk])
        ki = posi
        dve(lambda: V.tensor_copy(out=ki[:], in_=kf[:]), r=[kk], w=["posi"])
        dve(lambda: V.tensor_copy(out=kf[:], in_=ki[:]), r=["posi"], w=[kk])
        dve(lambda: V.scalar_tensor_tensor(out=ang[:], in0=kf[:], scalar=-C1, in1=ang[:], op0=ALU.mult, op1=ALU.add), r=[kk, ka], w=[ka])
        dve(lambda: V.scalar_tensor_tensor(out=ang[:], in0=kf[:], scalar=-C2, in1=ang[:], op0=ALU.mult, op1=ALU.add), r=[kk, ka], w=[ka])
        act(lambda: A.activation(out=sin_t[:], in_=ang[:], func=AF.Sin), r=[ka], w=["sin_t"])
        act(lambda: A.activation(out=kf[:], in_=ang[:], func=AF.Abs), r=[ka], w=[kk])
        act(lambda: A.activation(out=cos_t[:], in_=kf[:], func=AF.Sin, scale=-1.0, bias=halfpi[:, 0:1]), r=[kk, "halfpi"], w=["cos_t"])
        tr.dma("sp", sin_d[:, tsl], sin_t[:], reads=["sin_t"], writes=[f"sind{t}"])
        tr.dma("sp", cos_d[:, tsl], cos_t[:], reads=["cos_t"], writes=[f"cosd{t}"])

    _stage(1)
    act(lambda: A.activation(out=csil[:], in_=ccol[:], func=AF.Silu), r=["ccol"], w=["csil"])
    dve(lambda: V.tensor_copy(out=csil_hi[:], in_=csil[:]), r=["csil"], w=["csil_hi"])
    dve(lambda: V.tensor_tensor(out=csil_lo[:], in0=csil[:], in1=csil_hi[:], op=ALU.subtract), r=["csil", "csil_hi"], w=["csil_lo"])
    _stage(2)
    wstream = []
    corder = [("q", OQ), ("q", OQ + 512), ("k", OK_), ("k", OK_ + 512), ("v", OV), ("v", OV + 512),
              ("g", OG), ("g", OG + 512), ("fb", OFB), ("fa", OFA), ("fg", OFG),
              ("sc", OSC), ("sh", OSH), ("sg", OSG), ("sb", OSB)]
    for l in range(L):
        for t in range(NT):
            for kind, c0 in corder:
                wstream.append(win_in[l, :, c0:c0 + 512])
            for cc in range(4):
                wstream.append(wout_in[l, :, cc * 512:(cc + 1) * 512])
    wstate = {"issued": 0, "cur": -1}
    per_layer = len(wstream) // L

    def w_issue(upto):
        lay_end = (wstate["cur"] // per_layer + 1) * per_layer - 1
        while wstate["issued"] <= min(upto, lay_end):
            i = wstate["issued"]
            b = i % NB
            tr.dma("pool", wbuf[b][:], wstream[i].rearrange("(kt p) n -> p kt n", p=128), writes=[f"w{b}"])
            wstate["issued"] += 1

    def w_next():
        wstate["cur"] += 1
        i = wstate["cur"]
        w_issue(i + NB - 1)
        return i % NB

    def fm_group(b, cg):
        pi = nxt("mm", NMM)
        for kt in range(16):
            pe(lambda: PE.matmul(mm[pi][:], lhsT=wbuf[b][:, kt, cg * 128:(cg + 1) * 128], rhs=hT[:, kt, :],
                                 start=(kt == 0), stop=(kt == 15)),
               r=[f"w{b}"] + [f"hT{kt}_{s}" for s in range(NS)], w=[f"mm{pi}"], inc=(kt == 15))
        return pi

    def tm_group(b, s, src, srckeys):
        pi = nxt("mm", NMM)
        for kt in range(16):
            pe(lambda: PE.matmul(mm[pi][:], lhsT=src(kt), rhs=wbuf[b][:, kt, :], start=(kt == 0), stop=(kt == 15)),
               r=[f"w{b}", srckeys(kt)], w=[f"mm{pi}"], inc=(kt == 15))
        return pi

    ssl = lambda s: slice(s * 128, (s + 1) * 128)

    def xkeys(t, s):
        return [f"xd{t}_{s}_{c}" for c in range(4)]

    def layer_setup(l):
        hi_rep = k_tm[:, 0:2, :].rearrange("p a (b m) -> p (a b) m", m=128)
        lo_rep = k_tm[:, 2:4, :].rearrange("p a (b m) -> p (a b) m", m=128)
        dve(lambda: V.tensor_copy(out=hi_rep, in_=csil_hi[:, :].unsqueeze(2).to_broadcast([128, 16, 128])), r=["csil_hi"], w=["ktm0", "ktm1"])
        dve(lambda: V.tensor_copy(out=lo_rep, in_=csil_lo[:, :].unsqueeze(2).to_broadcast([128, 16, 128])), r=["csil_lo"], w=["ktm2", "ktm3"])
        brow = xin[0]
        scl, kscl = sclb, "sclb"
        for part in range(3):
            tr.dma("sp", brow[:], adab_in[l:l + 1, part * D:(part + 1) * D].partition_broadcast(128), writes=["xin0"])
            for cc in range(4):
                c0 = part * D + cc * 512
                csl = slice(cc * 512, (cc + 1) * 512)
                pi = nxt("mm", NMM)
                b = nxt("w", NB)
                tr.dma("pool", wbuf[b][:], adaw_in[l, :, c0:c0 + 512].rearrange("(kt p) n -> p kt n", p=128), writes=[f"w{b}"])
                for kt in range(16):
                    pe(lambda: PE.matmul(mm[pi][:], lhsT=hi_rep[:, kt, :], rhs=wbuf[b][:, kt, :], start=(kt == 0), stop=False),
                       r=["ktm0", "ktm1", f"w{b}"], w=[f"mm{pi}"], inc=False)
                    pe(lambda: PE.matmul(mm[pi][:], lhsT=lo_rep[:, kt, :], rhs=wbuf[b][:, kt, :], start=False, stop=(kt == 15)),
                       r=["ktm2", "ktm3", f"w{b}"], w=[f"mm{pi}"], inc=(kt == 15))
                if part == 2:
                    dve(lambda: V.tensor_tensor(out=gate_bc[:, csl], in0=mm[pi][:], in1=brow[:, csl], op=ALU.add),
                        r=[f"mm{pi}", "xin0"], w=["gate_bc"])
                else:
                    tt_, ktt = gettmp()
                    dst, kdst = (shiftc, "shiftc") if part == 0 else (scl, kscl)
                    dve(lambda: V.tensor_tensor(out=tt_[:], in0=mm[pi][:], in1=brow[:, csl], op=ALU.add), r=[f"mm{pi}", "xin0"], w=[ktt])
                    dve(lambda: V.tensor_tensor(out=tt_[:].rearrange("p (a m) -> p a m", m=128), in0=tt_[:].rearrange("p (a m) -> p a m", m=128),
                                                in1=identf[:, :].unsqueeze(1).to_broadcast([128, 4, 128]), op=ALU.mult),
                        r=[ktt, "identf"], w=[ktt])
                    dve(lambda: V.reduce_sum(out=dst[:, cc * 4:(cc + 1) * 4], in_=tt_[:].rearrange("p (a m) -> p a m", m=128), axis=AX.X),
                        r=[ktt], w=[kdst])
        dve(lambda: V.scalar_tensor_tensor(out=gs[:], in0=scl[:, 0:16], scalar=1.0, in1=normg[:, l, :], op0=ALU.add, op1=ALU.mult),
            r=[kscl, "normg"], w=["gs"])
        for h in range(NH):
            dve(lambda: V.memset(S[:, 2 * h:2 * h + 2, :], 0.0), w=[f"S{h}"])
            pool(lambda: P.memset(Sbf[:, 2 * h:2 * h + 2, :], 0.0), w=[f"Sbf{h}"])
        for ct in range(4):
            dve(lambda: V.memset(u_ext[:, ct, 0:2], 0.0), w=[f"u{ct}"])
            dve(lambda: V.memset(glu_ext[:, ct, 0:30], 0.0), w=[f"glu{ct}"])

    def item_norm(l, t):
        xsrc = x_in if l == 0 else x_s
        tr.dma("sp", cos_t[:], cos_d[:, t * TT:(t + 1) * TT], reads=[f"cosd{t}"], writes=["cos_t"])
        tr.dma("sp", sin_t[:], sin_d[:, t * TT:(t + 1) * TT], reads=[f"sind{t}"], writes=["sin_t"])
        for s in range(NS):
            xb, xk = xin[0], "xin0"
            r0 = t * TT + s * 128
            tr.dma("sp", xb[:], xsrc[r0:r0 + 128, :], reads=(xkeys(t, s) if l > 0 else []), writes=[xk])
            dve(lambda: V.memset(ss[:, 0:1], 0.0), w=["ss"])
            act(lambda: A.activation(out=hb[:], in_=xb[:], func=AF.Square, accum_out=ss[:, 0:1]), r=[xk, "ss"], w=["hb", "ss"])
            act(lambda: A.activation(out=ss[:, 1:2], in_=ss[:, 0:1], func=AF.Sqrt, scale=1.0 / D, bias=epsb[:, 0:1]), r=["ss", "epsb"], w=["ss"])
            dve(lambda: V.reciprocal(out=ss[:, 2:3], in_=ss[:, 1:2]), r=["ss"], w=["ss"])
            act(lambda: A.activation(out=hb[:], in_=xb[:], func=AF.Copy, scale=ss[:, 2:3]), r=[xk, "ss"], w=["hb"])
            _stage(3.3)
            for g in range(4):
                ti = nxt("tp", 2)
                for j in range(4):
                    kt = g * 4 + j
                    pe(lambda: PE.transpose(out=tp[ti][:, j * 128:(j + 1) * 128], in_=hb[:, kt * 128:(kt + 1) * 128], identity=ident[:]),
                       r=["hb", "ident"], w=[f"tp{ti}"], inc=(j == 3))
                _stage(3.6)
                for j in range(4):
                    kt = g * 4 + j
                    if False:
                        act(lambda: A.activation(out=hT[:, kt, ssl(s)], in_=tp[ti][:, j * 128:(j + 1) * 128], func=AF.Identity,
                                                 scale=gs[:, kt:kt + 1], bias=shiftc[:, kt:kt + 1]),
                            r=[f"tp{ti}", "gs", "shiftc"], w=[f"hT{kt}_{s}"])
                    else:
                        dve(lambda: V.tensor_scalar(out=hT[:, kt, ssl(s)], in0=tp[ti][:, j * 128:(j + 1) * 128],
                                                    scalar1=gs[:, kt:kt + 1], scalar2=shiftc[:, kt:kt + 1], op0=ALU.mult, op1=ALU.add),
                            r=[f"tp{ti}", "gs", "shiftc"], w=[f"hT{kt}_{s}"])
                    _stage(3.7 + 0.01 * j + 0.04 * g)

    def rotary(p1, p2, h, is_q):
        ra, kra = gettmp()
        rb, krb = gettmp()
        rc, krc = gettmp()
        rd, krd = gettmp()
        dve(lambda: V.tensor_tensor(out=ra[:], in0=mm[p1][:], in1=cos_t[:], op=ALU.mult), r=[f"mm{p1}", "cos_t"], w=[kra])
        dve(lambda: V.tensor_tensor(out=rb[:], in0=mm[p2][:], in1=sin_t[:], op=ALU.mult), r=[f"mm{p2}", "sin_t"], w=[krb])
        dve(lambda: V.tensor_tensor(out=rc[:], in0=mm[p2][:], in1=cos_t[:], op=ALU.mult), r=[f"mm{p2}", "cos_t"], w=[krc])
        dve(lambda: V.tensor_tensor(out=rd[:], in0=mm[p1][:], in1=sin_t[:], op=ALU.mult), r=[f"mm{p1}", "sin_t"], w=[krd])
        _stage(4.5)
        if is_q:
            qd = qdec[:, h:h + 1, :].to_broadcast([128, NS, 128])
            pool(lambda: P.tensor_tensor(out=ra[:], in0=ra[:], in1=rb[:], op=ALU.subtract), r=[kra, krb], w=[kra])
            pool(lambda: P.tensor_tensor(out=rc[:], in0=rc[:], in1=rd[:], op=ALU.add), r=[krc, krd], w=[krc])
            _stage(4.7)
            pool(lambda: P.tensor_tensor(out=qT[:, 2 * h, :].rearrange("p (s i) -> p s i", s=NS),
                                         in0=ra[:].rearrange("p (s i) -> p s i", s=NS), in1=qd, op=ALU.mult),
                 r=[kra, "cst"], w=[f"qT{2 * h}"])
            pool(lambda: P.tensor_tensor(out=qT[:, 2 * h + 1, :].rearrange("p (s i) -> p s i", s=NS),
                                         in0=rc[:].rearrange("p (s i) -> p s i", s=NS), in1=qd, op=ALU.mult),
                 r=[krc, "cst"], w=[f"qT{2 * h + 1}"])
        else:
            pool(lambda: P.tensor_tensor(out=kT[:, 2 * h, :], in0=ra[:], in1=rb[:], op=ALU.subtract), r=[kra, krb], w=[f"kT{2 * h}"])
            pool(lambda: P.tensor_tensor(out=kT[:, 2 * h + 1, :], in0=rc[:], in1=rd[:], op=ALU.add), r=[krc, krd], w=[f"kT{2 * h + 1}"])

    def item_qk(is_q, ci):
        b = w_next()
        for hh in range(2):
            h = 2 * ci + hh
            p1 = fm_group(b, 2 * hh)
            p2 = fm_group(b, 2 * hh + 1)
            _stage(4.2)
            rotary(p1, p2, h, is_q)

    def item_ktrans():
        for s in range(NS):
            ti = nxt("tp", 2)
            for j in range(8):
                pe(lambda: PE.transpose(out=tp[ti][:, j * 128:(j + 1) * 128], in_=kT[:, j, ssl(s)], identity=ident[:]),
                   r=[f"kT{j}", "ident"], w=[f"tp{ti}"], inc=(j == 7))
            dve(lambda: V.tensor_tensor(out=k_tm[:, s, :].rearrange("p (h d) -> p h d", h=NH),
                                        in0=tp[ti][:].rearrange("p (h d) -> p h d", h=NH),
                                        in1=kdec.unsqueeze(2).to_broadcast([128, NH, HD]), op=ALU.mult),
                r=[f"tp{ti}", "cst"], w=[f"ktm{s}"])

    def item_vg(is_v, ci):
        b = w_next()
        for s in range(NS):
            pi = tm_group(b, s, lambda kt: hT[:, kt, ssl(s)], lambda kt: f"hT{kt}_{s}")
            if is_v:
                act(lambda: A.activation(out=v_tm[:, s, ci * 512:(ci + 1) * 512], in_=mm[pi][:], func=AF.Copy),
                    r=[f"mm{pi}"], w=[f"v{s}_{ci}"])
            else:
                act(lambda: A.activation(out=sg[:, s, ci * 512:(ci + 1) * 512], in_=mm[pi][:], func=AF.Silu),
                    r=[f"mm{pi}"], w=[f"sg{s}_{ci}"])

    def item_cf(l, kind):
        b = w_next()
        for ct in range(4):
            pi = fm_group(b, ct)
            if kind == "fb":
                act(lambda: A.activation(out=cfy[:, ct, :], in_=mm[pi][:], func=AF.Sigmoid), r=[f"mm{pi}"], w=[f"cfy{ct}"])
            elif kind == "fa":
                dve(lambda: V.tensor_tensor(out=glu_ext[:, ct, 30:30 + TT], in0=mm[pi][:], in1=cfy[:, ct, :], op=ALU.mult),
                    r=[f"mm{pi}", f"cfy{ct}"], w=[f"glu{ct}"])
                dve(lambda: V.tensor_scalar(out=cfy[:, ct, :], in0=glu_ext[:, ct, 0:TT], scalar1=cfw[:, l, ct, 0:1],
                                            scalar2=cfp[:, l, 0, ct:ct + 1], op0=ALU.mult, op1=ALU.add),
                    r=[f"glu{ct}", "cfw", "cfp"], w=[f"cfy{ct}"])
                for k in range(1, CFW):
                    dve(lambda: V.scalar_tensor_tensor(out=cfy[:, ct, :], in0=glu_ext[:, ct, k:k + TT], scalar=cfw[:, l, ct, k:k + 1],
                                                       in1=cfy[:, ct, :], op0=ALU.mult, op1=ALU.add),
                        r=[f"glu{ct}", "cfw", f"cfy{ct}"], w=[f"cfy{ct}"])
                pool(lambda: P.tensor_copy(out=glu_ext[:, ct, 0:30], in_=glu_ext[:, ct, TT:TT + 30]), r=[f"glu{ct}"], w=[f"glu{ct}"])
            else:
                sgt, ksg = gettmp()
                act(lambda: A.activation(out=sgt[:], in_=mm[pi][:], func=AF.Silu), r=[f"mm{pi}"], w=[ksg])
                pool(lambda: P.tensor_tensor(out=mixedT[:, 12 + ct, :], in0=cfy[:, ct, :], in1=sgt[:], op=ALU.mult),
                     r=[f"cfy{ct}", ksg], w=[f"mx{12 + ct}"])
        if kind == "fa":
            pm = nxt("mm", NMM)
            pq = nxt("mm", NMM)
            for ct in range(4):
                t1, k1 = gettmp()
                t2, k2 = gettmp()
                hi = t1[:].bitcast(BF16)[:, 0:TT]
                lo = t1[:].bitcast(BF16)[:, TT:2 * TT]
                sq = t2[:].bitcast(BF16)[:, 0:TT]
                act(lambda: A.activation(out=hi, in_=cfy[:, ct, :], func=AF.Copy), r=[f"cfy{ct}"], w=[k1])
                pool(lambda: P.tensor_tensor(out=lo, in0=cfy[:, ct, :], in1=hi, op=ALU.subtract), r=[f"cfy{ct}", k1], w=[k1])
                act(lambda: A.activation(out=sq, in_=cfy[:, ct, :], func=AF.Square), r=[f"cfy{ct}"], w=[k2])
                pe(lambda: PE.matmul(mm[pm][:], lhsT=ones_b[:], rhs=hi, start=(ct == 0), stop=False),
                   r=["ones_b", k1], w=[f"mm{pm}"], inc=False)
                pe(lambda: PE.matmul(mm[pm][:], lhsT=ones_b[:], rhs=lo, start=False, stop=(ct == 3)),
                   r=["ones_b", k1], w=[f"mm{pm}"], inc=(ct == 3))
                pe(lambda: PE.matmul(mm[pq][:], lhsT=ones_b[:], rhs=sq, start=(ct == 0), stop=(ct == 3)),
                   r=["ones_b", k2], w=[f"mm{pq}"], inc=(ct == 3))
            act(lambda: A.activation(out=mean_sb[:], in_=mm[pm][:], func=AF.Copy), r=[f"mm{pm}"], w=["mean_sb"])
            pool(lambda: P.tensor_tensor(out=rstd_sb[:], in0=mean_sb[:], in1=mean_sb[:], op=ALU.mult), r=["mean_sb"], w=["rstd_sb"])
            dve(lambda: V.tensor_tensor(out=rstd_sb[:], in0=mm[pq][:], in1=rstd_sb[:], op=ALU.subtract), r=[f"mm{pq}", "rstd_sb"], w=["rstd_sb"])
            act(lambda: A.activation(out=rstd_sb[:], in_=rstd_sb[:], func=AF.Sqrt, bias=epsb[:, 0:1]), r=["rstd_sb", "epsb"], w=["rstd_sb"])
            dve(lambda: V.reciprocal(out=rstd_sb[:], in_=rstd_sb[:]), r=["rstd_sb"], w=["rstd_sb"])
            for ct in range(4):
                pool(lambda: P.tensor_tensor(out=cfy[:, ct, :], in0=cfy[:, ct, :], in1=mean_sb[:], op=ALU.subtract),
                     r=[f"cfy{ct}", "mean_sb"], w=[f"cfy{ct}"])
                pool(lambda: P.tensor_tensor(out=cfy[:, ct, :], in0=cfy[:, ct, :], in1=rstd_sb[:], op=ALU.mult),
                     r=[f"cfy{ct}", "rstd_sb"], w=[f"cfy{ct}"])
                act(lambda: A.activation(out=cfy[:, ct, :], in_=cfy[:, ct, :], func=AF.Silu, scale=cfp[:, l, 1, ct:ct + 1],
                                         bias=cfp[:, l, 2, ct:ct + 1]), r=[f"cfy{ct}", "cfp"], w=[f"cfy{ct}"])

    def item_sc(l, kind):
        b = w_next()
        for ct in range(4):
            pi = fm_group(b, ct)
            if kind == "sc":
                act(lambda: A.activation(out=scy[:, ct, :], in_=mm[pi][:], func=AF.Copy), r=[f"mm{pi}"], w=[f"scy{ct}"])
            elif kind == "sh":
                dve(lambda: V.tensor_tensor(out=u_ext[:, ct, 2:2 + TT], in0=mm[pi][:], in1=scy[:, ct, :], op=ALU.mult),
                    r=[f"mm{pi}", f"scy{ct}"], w=[f"u{ct}"])
                dve(lambda: V.tensor_scalar(out=scy[:, ct, :], in0=u_ext[:, ct, 0:TT], scalar1=scw[:, l, ct, 0:1], scalar2=None, op0=ALU.mult),
                    r=[f"u{ct}", "scw"], w=[f"scy{ct}"])
                for k in range(1, SCW):
                    dve(lambda: V.scalar_tensor_tensor(out=scy[:, ct, :], in0=u_ext[:, ct, k:k + TT], scalar=scw[:, l, ct, k:k + 1],
                                                       in1=scy[:, ct, :], op0=ALU.mult, op1=ALU.add),
                        r=[f"u{ct}", "scw", f"scy{ct}"], w=[f"scy{ct}"])
                pool(lambda: P.tensor_copy(out=u_ext[:, ct, 0:2], in_=u_ext[:, ct, TT:TT + 2]), r=[f"u{ct}"], w=[f"u{ct}"])
            elif kind == "sg":
                sgt, ksg = gettmp()
                act(lambda: A.activation(out=sgt[:], in_=mm[pi][:], func=AF.Silu), r=[f"mm{pi}"], w=[ksg])
                pool(lambda: P.tensor_tensor(out=scy[:, ct, :], in0=scy[:, ct, :], in1=sgt[:], op=ALU.mult), r=[f"scy{ct}", ksg], w=[f"scy{ct}"])
            else:
                dve(lambda: V.tensor_tensor(out=mixedT[:, 8 + ct, :], in0=mm[pi][:], in1=scy[:, ct, :], op=ALU.mult),
                    r=[f"mm{pi}", f"scy{ct}"], w=[f"mx{8 + ct}"])

    def item_ret():
        for s in range(NS):
            for h in range(NH):
                for half in range(2):
                    pe(lambda: PE.matmul(sc_ps[:, h * 128:(h + 1) * 128], lhsT=kT[:, 2 * h + half, ssl(s)], rhs=qT[:, 2 * h + half, ssl(s)],
                                         start=(half == 0), stop=(half == 1)),
                       r=[f"kT{2 * h + half}", f"qT{2 * h + half}"], w=["sc_ps"], inc=(h == NH - 1 and half == 1))
            dve(lambda: V.tensor_tensor(out=sT[:], in0=sc_ps[:].rearrange("p (h i) -> p h i", h=NH), in1=maskT, op=ALU.mult),
                r=["sc_ps", "cst"], w=["sT"])
            for hp in range(2):
                for hh in range(2):
                    h = 2 * hp + hh
                    osl = slice(0, 256)
                    o_ps = o_pss[hh]
                    vsl = slice(h * 256, (h + 1) * 256)
                    pe(lambda: PE.matmul(o_ps[:, osl], lhsT=sT[:, h, :], rhs=v_tm[:, s, vsl], start=True, stop=False),
                       r=["sT", f"v{s}_{h // 2}"], w=[f"o{hh}"], inc=False)
                    pe(lambda: PE.matmul(o_ps[:, osl], lhsT=qT[:, 2 * h, ssl(s)], rhs=Sbf[:, 2 * h, :], start=False, stop=False),
                       r=[f"qT{2 * h}", f"Sbf{h}"], w=[f"o{hh}"], inc=False)
                    pe(lambda: PE.matmul(o_ps[:, osl], lhsT=qT[:, 2 * h + 1, ssl(s)], rhs=Sbf[:, 2 * h + 1, :], start=False, stop=True),
                       r=[f"qT{2 * h + 1}", f"Sbf{h}"], w=[f"o{hh}"])
                    dve(lambda: V.bn_stats(out=st6[:, h, :], in_=o_ps[:, osl]), r=[f"o{hh}"], w=[f"st{h}"])
                    dve(lambda: V.bn_aggr(out=mv[:, h, :], in_=st6[:, h, :]), r=[f"st{h}"], w=[f"mv{h}"])
                    act(lambda: A.activation(out=gn[:, h, 0:1], in_=mv[:, h, 1:2], func=AF.Sqrt, bias=epsb[:, 0:1]), r=[f"mv{h}", "epsb"], w=[f"gn{h}"])
                    dve(lambda: V.reciprocal(out=gn[:, h, 1:2], in_=gn[:, h, 0:1]), r=[f"gn{h}"], w=[f"gn{h}"])
                    dve(lambda: V.scalar_tensor_tensor(out=gn[:, h, 2:3], in0=mv[:, h, 0:1], scalar=-1.0, in1=gn[:, h, 1:2],
                                                       op0=ALU.mult, op1=ALU.mult), r=[f"mv{h}", f"gn{h}"], w=[f"gn{h}"])
                    dve(lambda: V.tensor_scalar(out=on[hh][:], in0=o_ps[:, osl], scalar1=gn[:, h, 1:2], scalar2=gn[:, h, 2:3], op0=ALU.mult, op1=ALU.add),
                        r=[f"o{hh}", f"gn{h}"], w=[f"on{hh}"])
                    pool(lambda: P.tensor_tensor(out=og[:, vsl], in0=on[hh][:], in1=sg[:, s, vsl], op=ALU.mult),
                         r=[f"on{hh}", f"sg{s}_{h // 2}"], w=[f"og{h}"])
            for h in range(NH):
                pi = nxt("mm", NMM)
                vsl = slice(h * 256, (h + 1) * 256)
                for half in range(2):
                    pe(lambda: PE.matmul(mm[pi][:, half * 256:(half + 1) * 256], lhsT=k_tm[:, s, h * 256 + half * 128:h * 256 + (half + 1) * 128],
                                         rhs=v_tm[:, s, vsl], start=True, stop=True),
                       r=[f"ktm{s}", f"v{s}_{h // 2}"], w=[f"mm{pi}"], inc=(half == 1))
                Sv = S[:, 2 * h:2 * h + 2, :].rearrange("p a b -> p (a b)")
                dve(lambda: V.scalar_tensor_tensor(out=Sv, in0=Sv, scalar=CDEC[h], in1=mm[pi][:], op0=ALU.mult, op1=ALU.add),
                    r=[f"S{h}", f"mm{pi}"], w=[f"S{h}"])
                pool(lambda: P.tensor_copy(out=Sbf[:, 2 * h:2 * h + 2, :], in_=S[:, 2 * h:2 * h + 2, :]), r=[f"S{h}"], w=[f"Sbf{h}"])
            ti = nxt("tp", 2)
            for j in range(8):
                pe(lambda: PE.transpose(out=tp[ti][:, j * 128:(j + 1) * 128], in_=og[:, j * 128:(j + 1) * 128], identity=ident[:]),
                   r=[f"og{j // 2}", "ident"], w=[f"tp{ti}"], inc=(j == 7))
            dve(lambda: V.tensor_copy(out=mixedT[:, 0:8, ssl(s)], in_=tp[ti][:].rearrange("p (j t) -> p j t", j=8)),
                r=[f"tp{ti}"], w=[f"mxr{s}"])

    def item_out(l, t):
        for cc in range(4):
            b = w_next()
            csl = slice(cc * 512, (cc + 1) * 512)
            for s in range(NS):
                pi = tm_group(b, s, lambda kt: mixedT[:, kt, ssl(s)], lambda kt: (f"mxr{s}" if kt < 8 else f"mx{kt}"))
                r0 = t * TT + s * 128
                xp, kxp = gettmp()
                gt, kgt = gettmp()
                xsrc = x_in if l == 0 else x_s
                tr.dma("sp", xp[:], xsrc[r0:r0 + 128, csl], reads=([f"xd{t}_{s}_{cc}"] if l > 0 else []), writes=[kxp])
                dve(lambda: V.tensor_tensor(out=gt[:], in0=mm[pi][:], in1=gate_bc[:, csl], op=ALU.mult), r=[f"mm{pi}", "gate_bc"], w=[kgt])
                pool(lambda: P.tensor_tensor(out=xp[:], in0=xp[:], in1=gt[:], op=ALU.add), r=[kxp, kgt], w=[kxp])
                tr.dma("sp", x_s[r0:r0 + 128, csl], xp[:], reads=[kxp], writes=[f"xd{t}_{s}_{cc}"])

    for l in range(L):
        layer_setup(l)
        for t in range(NT):
            _stage(3)
            item_norm(l, t)
            _stage(4)
            item_qk(True, 0)
            item_qk(True, 1)
            _stage(5)
            item_qk(False, 0)
            item_qk(False, 1)
            item_ktrans()
            _stage(6)
            item_vg(True, 0)
            item_vg(True, 1)
            item_vg(False, 0)
            item_vg(False, 1)
            _stage(7)
            item_cf(l, "fb")
            item_cf(l, "fa")
            item_cf(l, "fg")
            _stage(8)
            item_sc(l, "sc")
            item_sc(l, "sh")
            item_sc(l, "sg")
            item_sc(l, "sb")
            _stage(9)
            item_ret()
            _stage(10)
            item_out(l, t)
            _stage(11)

    tr.dma("sp", gate_bc[:], fg_in.partition_broadcast(128), writes=["gate_bc"])
    for t in range(NT):
        for s in range(NS):
            xb, xk = xin[0], "xin0"
            r0 = t * TT + s * 128
            tr.dma("sp", xb[:], x_s[r0:r0 + 128, :], reads=xkeys(t, s), writes=[xk])
            dve(lambda: V.memset(ss[:, 0:1], 0.0), w=["ss"])
            act(lambda: A.activation(out=hb[:], in_=xb[:], func=AF.Square, accum_out=ss[:, 0:1]), r=[xk, "ss"], w=["hb", "ss"])
            act(lambda: A.activation(out=ss[:, 1:2], in_=ss[:, 0:1], func=AF.Sqrt, scale=1.0 / D, bias=epsb[:, 0:1]), r=["ss", "epsb"], w=["ss"])
            dve(lambda: V.reciprocal(out=ss[:, 2:3], in_=ss[:, 1:2]), r=["ss"], w=["ss"])
            dve(lambda: V.scalar_tensor_tensor(out=xb[:], in0=xb[:], scalar=ss[:, 2:3], in1=gate_bc[:], op0=ALU.mult, op1=ALU.mult),
                r=[xk, "ss", "gate_bc"], w=[xk])
            tr.dma("sp", out[r0:r0 + 128, :], xb[:], reads=[xk], writes=[f"out{t}_{s}"])
    tr.wait_all("sp")


def host_consts():
    i = np.arange(128, dtype=np.float64)
    maskT = np.zeros((128, NH, 128), np.float64)
    qdec = np.zeros((128, NH, 128), np.float64)
    kdec = np.zeros((128, NH), np.float64)
    for h in range(NH):
        g = GAM[h]
        maskT[:, h, :] = (i[None, :] >= i[:, None]) * (g ** (-(i[:, None] + 1.0))) / 16.0
        qdec[:, h, :] = (g ** (i + 1.0))[None, :]
        kdec[:, h] = g ** (127.0 - i) / 16.0
    invf = 10000.0 ** (-np.arange(128, dtype=np.float32) / np.float32(128))
    cst = np.concatenate([maskT.reshape(128, -1), qdec.reshape(128, -1), kdec, invf.astype(np.float64)[:, None]], axis=1)
    return np.ascontiguousarray(cst.astype(np.float32))


def make_in_maps(x, c, positions, ada_w, ada_b, norm_g, w_in, sc_conv_w, cf_conv_w, cf_conv_b, cf_ln_g, cf_ln_b, w_out, final_g):
    B, T, _ = x.shape
    L = w_in.shape[0]
    f = lambda a: np.ascontiguousarray(np.asarray(a, dtype=np.float32))
    cst = host_consts()
    normg = f(np.asarray(norm_g).reshape(L, 16, 128).transpose(2, 0, 1))
    scw = f(np.asarray(sc_conv_w).reshape(L, SCW, 4, 128).transpose(3, 0, 2, 1))
    cfw = f(np.asarray(cf_conv_w).reshape(L, CFW, 4, 128).transpose(3, 0, 2, 1))
    cfp = f(np.stack([np.asarray(cf_conv_b), np.asarray(cf_ln_g), np.asarray(cf_ln_b)], axis=1).reshape(L, 3, 4, 128).transpose(3, 0, 1, 2))
    shared = {"ada_w": f(ada_w), "ada_b": f(ada_b), "normg": normg, "w_in": f(w_in), "w_out": f(w_out), "scw": scw,
              "cfw": cfw, "cfp": cfp, "final_g": f(final_g).reshape(1, D), "cst": cst}
    maps = []
    for b in range(B):
        m = dict(shared)
        m["x"] = f(x[b])
        m["pos"] = np.ascontiguousarray(np.asarray(positions[b], dtype=np.int32).reshape(1, T))
        m["ccol"] = f(np.asarray(c[b]).reshape(16, 128).T)
        maps.append(m)
    return maps


_NC_CACHE = {}


def kernel(x, c, positions, ada_w, ada_b, norm_g, w_in, sc_conv_w, cf_conv_w, cf_conv_b, cf_ln_g, cf_ln_b, w_out, final_g):
    x = np.asarray(x)
    B, T, _ = x.shape
    L = np.asarray(w_in).shape[0]
    maps = make_in_maps(x, c, positions, ada_w, ada_b, norm_g, w_in, sc_conv_w, cf_conv_w, cf_conv_b, cf_ln_g, cf_ln_b, w_out, final_g)
    key = (T, L)
    if key not in _NC_CACHE:
        _NC_CACHE[key] = build_nc(T, L)
    nc = _NC_CACHE[key]
    res = run_bass_kernel_spmd(nc, maps, core_ids=list(range(B)))
    return np.stack([np.asarray(res.results[b]["out"]) for b in range(B)], axis=0).astype(np.float32)
```

```python
import math
import numpy as np
import concourse.bass as bass
import concourse.mybir as mybir
from concourse.bass_utils import run_bass_kernel_spmd

F32 = mybir.dt.float32
BF16 = mybir.dt.bfloat16
I32 = mybir.dt.int32
AF = mybir.ActivationFunctionType
ALU = mybir.AluOpType
AX = mybir.AxisListType

D = 2048
DIN = 7680
NH = 4
HD = 256
SCW = 3
CFW = 31
EPS = 1e-6
TT = 512
NS = 4
NB = 2
NTMP = 8
GAM = [1.0 - 2.0 ** (-5 - h) for h in range(NH)]
CDEC = [g ** 128 for g in GAM]
C1 = 6.28125
C2 = 2 * math.pi - 6.28125

OQ, OK_, OV, OG = 0, 1024, 2048, 3072
OSB, OSC, OSH, OSG = 4096, 4608, 5120, 5632
OFA, OFB, OFG = 6144, 6656, 7168


class TR:
    def __init__(self, nc, ndma=8):
        self.nc = nc
        self.eng = {"pe": nc.tensor, "act": nc.scalar, "dve": nc.vector, "pool": nc.gpsimd, "sp": nc.sync}
        self.sem = {k: nc.alloc_semaphore(name="s_" + k) for k in self.eng}
        self.cnt = {k: 0 for k in self.eng}
        self.seen = {k: {} for k in self.eng}
        self.dsem = [nc.alloc_semaphore(name=f"dq{i}") for i in range(2 * ndma)]
        self.dcnt = [0] * (2 * ndma)
        self.ndma = ndma
        self.dnext = {"hw": 0, "sw": 0}
        self.lastw = {}
        self.reads = {}
        self.same = {"pe": False, "act": True, "dve": True, "pool": True, "sp": True}

    def _wait(self, e, tok):
        if tok is None:
            return
        kind, key, val = tok
        if kind == "e" and key == e and not self.same[e]:
            return
        sk = (kind, key)
        if self.seen[e].get(sk, 0) >= val:
            return
        self.seen[e][sk] = val
        sem = self.sem[key] if kind == "e" else self.dsem[key]
        self.eng[e].wait_ge(sem, val)

    def deps(self, e, reads, writes):
        for b in reads:
            self._wait(e, self.lastw.get(b))
        for b in writes:
            self._wait(e, self.lastw.get(b))
            for sk, v in self.reads.get(b, {}).items():
                self._wait(e, (sk[0], sk[1], v))

    def commit(self, tok, reads, writes):
        sk = (tok[0], tok[1])
        for b in reads:
            d = self.reads.setdefault(b, {})
            if d.get(sk, 0) < tok[2]:
                d[sk] = tok[2]
        for b in writes:
            self.lastw[b] = tok
            self.reads[b] = {}

    def op(self, e, fn, reads=(), writes=(), inc=True):
        self.deps(e, reads, writes)
        ins = fn()
        if inc:
            self.cnt[e] += 1
            ins.then_inc(self.sem[e], 1)
            tok = ("e", e, self.cnt[e])
        else:
            tok = ("e", e, self.cnt[e] + 1)
        self.commit(tok, reads, writes)
        return tok

    def dma(self, e, out, in_, reads=(), writes=(), **kw):
        self.deps(e, reads, writes)
        cls = "sw" if e == "pool" else "hw"
        i = self.dnext[cls] + (self.ndma if cls == "sw" else 0)
        self.dnext[cls] = (self.dnext[cls] + 1) % self.ndma
        if self.dcnt[i] > 0:
            self._wait(e, ("d", i, self.dcnt[i]))
        self.eng[e].dma_start(out=out, in_=in_, **kw).then_inc(self.dsem[i], 16)
        self.dcnt[i] += 16
        tok = ("d", i, self.dcnt[i])
        self.commit(tok, reads, writes)
        return tok

    def wait_all(self, e):
        for k in self.eng:
            if self.cnt[k] > 0:
                self._wait(e, ("e", k, self.cnt[k]))
        for i in range(len(self.dsem)):
            if self.dcnt[i] > 0:
                self._wait(e, ("d", i, self.dcnt[i]))


class _Stop(Exception):
    pass


DEBUG_STOP = None


def _stage(n):
    if DEBUG_STOP is not None and n >= DEBUG_STOP:
        raise _Stop()


def build_nc(T, L):
    nc_holder = {}
    try:
        _build_nc(T, L, nc_holder)
    except _Stop:
        nc_holder["tr"].wait_all("sp")
    return nc_holder["nc"]


def _build_nc(T, L, nc_holder):
    NT = T // TT
    nc = bass.Bass("TRN2", target_bir_lowering=False)
    dt_in = lambda name, shape, dt=F32: nc.dram_tensor(name, list(shape), dt, kind="ExternalInput").ap()
    x_in = dt_in("x", [T, D])
    pos_in = dt_in("pos", [1, T], I32)
    ccol_in = dt_in("ccol", [128, 16])
    adaw_in = dt_in("ada_w", [L, D, 3 * D])
    adab_in = dt_in("ada_b", [L, 3 * D])
    normg_in = dt_in("normg", [128, L, 16])
    win_in = dt_in("w_in", [L, D, DIN])
    wout_in = dt_in("w_out", [L, D, D])
    scw_in = dt_in("scw", [128, L, 4, SCW])
    cfw_in = dt_in("cfw", [128, L, 4, CFW])
    cfp_in = dt_in("cfp", [128, L, 3, 4])
    fg_in = dt_in("final_g", [1, D])
    cst_in = dt_in("cst", [128, 4 * 128 + 4 * 128 + 4 + 1])
    out = nc.dram_tensor("out", [T, D], F32, kind="ExternalOutput").ap()
    x_s = nc.dram_tensor("x_s", [T, D], F32).ap()
    cos_d = nc.dram_tensor("cos_d", [128, T], F32).ap()
    sin_d = nc.dram_tensor("sin_d", [128, T], F32).ap()

    tr = TR(nc)
    nc_holder["nc"] = nc
    nc_holder["tr"] = tr
    sb = lambda name, shape, dt=F32: nc.alloc_sbuf_tensor("sb_" + name, list(shape), dt)
    wbuf = [sb(f"wbuf{i}", [128, 16, 512], BF16) for i in range(NB)]
    xin = [sb("xin0", [128, D])]
    hb = sb("hb", [128, D], BF16)
    hT = sb("hT", [128, 16, TT], BF16)
    gate_bc = sb("gate_bc", [128, D])
    cos_t = sb("cos_t", [128, TT])
    sin_t = sb("sin_t", [128, TT])
    qT = sb("qT", [128, 8, TT], BF16)
    kT = sb("kT", [128, 8, TT], BF16)
    k_tm = sb("k_tm", [128, NS, 1024], BF16)
    v_tm = sb("v_tm", [128, NS, 1024], BF16)
    sg = sb("sg", [128, NS, 1024], BF16)
    S = sb("S", [128, 8, 256])
    Sbf = sb("Sbf", [128, 8, 256], BF16)
    mixedT = sb("mixedT", [128, 16, TT], BF16)
    tmp = [sb(f"tmp{i}", [128, TT]) for i in range(NTMP)]
    sT = sb("sT", [128, 4, 128], BF16)
    on = [sb(f"on{i}", [128, 256]) for i in range(2)]
    og = sb("og", [128, 1024], BF16)
    st6 = sb("st6", [128, 4, 6])
    mv = sb("mv", [128, 4, 2])
    gn = sb("gn", [128, 4, 3])
    ss = sb("ss", [128, 4])
    scy = sb("scy", [128, 4, TT])
    u_ext = sb("u_ext", [128, 4, TT + 2])
    cfy = sb("cfy", [128, 4, TT])
    glu_ext = sb("glu_ext", [128, 4, TT + 30])
    mean_sb = sb("mean_sb", [128, TT])
    rstd_sb = sb("rstd_sb", [128, TT])
    ident = sb("ident", [128, 128], BF16)
    identf = sb("identf", [128, 128])
    ones_b = sb("ones_b", [128, 128], BF16)
    cst = sb("cst", [128, 4 * 128 + 4 * 128 + 4 + 1])
    halfpi = sb("halfpi", [128, 1])
    epsb = sb("epsb", [128, 1])
    normg = sb("normg", [128, L, 16])
    scw = sb("scw", [128, L, 4, SCW])
    cfw = sb("cfw", [128, L, 4, CFW])
    cfp = sb("cfp", [128, L, 3, 4])
    ccol = sb("ccol", [128, 16])
    csil = sb("csil", [128, 16])
    csil_hi = sb("csil_hi", [128, 16], BF16)
    csil_lo = sb("csil_lo", [128, 16], BF16)
    gs = sb("gs", [128, 16])
    sclb = sb("sclb", [128, 16])
    shiftc = sb("shiftc", [128, 16])
    posi = sb("posi", [128, TT], I32)
    maskT = cst[:, 0:512].rearrange("p (h i) -> p h i", h=4)
    qdec = cst[:, 512:1024].rearrange("p (h i) -> p h i", h=4)
    kdec = cst[:, 1024:1028]
    invf = cst[:, 1028:1029]
    NMM = 3
    mm = [nc.alloc_psum_tensor(f"ps_mm{i}", [128, 512], F32) for i in range(NMM)]
    tp = [nc.alloc_psum_tensor(f"ps_tp{i}", [128, 1024], BF16) for i in range(2)]
    sc_ps = nc.alloc_psum_tensor("ps_sc", [128, 512], F32)
    o_pss = [nc.alloc_psum_tensor(f"ps_o{i}", [128, 512], F32) for i in range(2)]

    import os as _os
    V, A, P, PE = nc.vector, nc.scalar, nc.gpsimd, nc.tensor
    cnt = {"mm": 0, "tp": 0, "tmp": 0, "w": 0, "alt": 0}

    def nxt(k, n):
        i = cnt[k] % n
        cnt[k] += 1
        return i

    def dve(fn, r=(), w=()):
        return tr.op("dve", fn, r, w)

    def act(fn, r=(), w=()):
        return tr.op("act", fn, r, w)

    def pool(fn, r=(), w=()):
        return tr.op("pool", fn, r, w)

    def pe(fn, r=(), w=(), inc=True):
        if _os.environ.get("ALLINC"):
            inc = True
        return tr.op("pe", fn, r, w, inc=inc)

    bg = []

    def bg_pump(n):
        for _ in range(min(n, len(bg))):
            bg.pop(0)()

    def gettmp():
        i = nxt("tmp", NTMP)
        return tmp[i], f"tmp{i}"

    tr.dma("sp", cst[:], cst_in, writes=["cst"])
    tr.dma("sp", normg[:], normg_in, writes=["normg"])
    tr.dma("sp", scw[:], scw_in, writes=["scw"])
    tr.dma("sp", cfw[:], cfw_in, writes=["cfw"])
    tr.dma("sp", cfp[:], cfp_in, writes=["cfp"])
    tr.dma("sp", ccol[:], ccol_in, writes=["ccol"])
    pool(lambda: P.memset(identf[:], 0.0), w=["identf"])
    pool(lambda: P.affine_select(out=identf[:], in_=identf[:], pattern=[[-1, 128]], compare_op=ALU.not_equal,
                                 fill=1.0, base=0, channel_multiplier=1), r=["identf"], w=["identf"])
    dve(lambda: V.tensor_copy(out=ident[:], in_=identf[:]), r=["identf"], w=["ident"])
    dve(lambda: V.memset(ones_b[:], 1.0 / 512.0), w=["ones_b"])
    dve(lambda: V.memset(halfpi[:], math.pi / 2), w=["halfpi"])
    dve(lambda: V.memset(epsb[:], EPS), w=["epsb"])

    _stage(0)
    for t in range(NT):
        tsl = slice(t * TT, (t + 1) * TT)
        tr.dma("sp", posi[:], pos_in[:, tsl].partition_broadcast(128), writes=["posi"])
        ang, ka = gettmp()
        kf, kk = gettmp()
        dve(lambda: V.tensor_copy(out=kf[:], in_=posi[:]), r=["posi"], w=[kk])
        dve(lambda: V.tensor_scalar(out=ang[:], in0=kf[:], scalar1=invf, scalar2=None, op0=ALU.mult), r=[kk, "cst"], w=[ka])
        dve(lambda: V.tensor_scalar(out=kf[:], in0=ang[:], scalar1=1.0 / (2 * math.pi), scalar2=None, op0=ALU.mult), r=[ka], w=[kk])
        ki = posi
        dve(lambda: V.tensor_copy(out=ki[:], in_=kf[:]), r=[kk], w=["posi"])
        dve(lambda: V.tensor_copy(out=kf[:], in_=ki[:]), r=["posi"], w=[kk])
        dve(lambda: V.scalar_tensor_tensor(out=ang[:], in0=kf[:], scalar=-C1, in1=ang[:], op0=ALU.mult, op1=ALU.add), r=[kk, ka], w=[ka])
        dve(lambda: V.scalar_tensor_tensor(out=ang[:], in0=kf[:], scalar=-C2, in1=ang[:], op0=ALU.mult, op1=ALU.add), r=[kk, ka], w=[ka])
        act(lambda: A.activation(out=sin_t[:], in_=ang[:], func=AF.Sin), r=[ka], w=["sin_t"])
        act(lambda: A.activation(out=kf[:], in_=ang[:], func=AF.Abs), r=[ka], w=[kk])
        act(lambda: A.activation(out=cos_t[:], in_=kf[:], func=AF.Sin, scale=-1.0, bias=halfpi[:, 0:1]), r=[kk, "halfpi"], w=["cos_t"])
        tr.dma("sp", sin_d[:, tsl], sin_t[:], reads=["sin_t"], writes=[f"sind{t}"])
        tr.dma("sp", cos_d[:, tsl], cos_t[:], reads=["cos_t"], writes=[f"cosd{t}"])

    _stage(1)
    act(lambda: A.activation(out=csil[:], in_=ccol[:], func=AF.Silu), r=["ccol"], w=["csil"])
    dve(lambda: V.tensor_copy(out=csil_hi[:], in_=csil[:]), r=["csil"], w=["csil_hi"])
    dve(lambda: V.tensor_tensor(out=csil_lo[:], in0=csil[:], in1=csil_hi[:], op=ALU.subtract), r=["csil", "csil_hi"], w=["csil_lo"])
    _stage(2)
    wstream = []
    corder = [("q", OQ), ("q", OQ + 512), ("k", OK_), ("k", OK_ + 512), ("v", OV), ("v", OV + 512),
              ("g", OG), ("g", OG + 512), ("fb", OFB), ("fa", OFA),
              ("sc", OSC), ("sh", OSH), ("sg", OSG), ("sb", OSB), ("fg", OFG)]
    for l in range(L):
        for t in range(NT):
            for kind, c0 in corder:
                wstream.append(win_in[l, :, c0:c0 + 512])
            for cc in range(4):
                wstream.append(wout_in[l, :, cc * 512:(cc + 1) * 512])
    wstate = {"issued": 0, "cur": -1}
    per_layer = len(wstream) // L

    def w_issue(upto):
        lay_end = (wstate["cur"] // per_layer + 1) * per_layer - 1
        while wstate["issued"] <= min(upto, lay_end):
            i = wstate["issued"]
            b = i % NB
            tr.dma("pool", wbuf[b][:], wstream[i].rearrange("(kt p) n -> p kt n", p=128), writes=[f"w{b}"])
            wstate["issued"] += 1

    def w_next():
        wstate["cur"] += 1
        i = wstate["cur"]
        w_issue(i + NB - 1)
        return i % NB

    def fm_group(b, cg):
        pi = nxt("mm", NMM)
        for kt in range(16):
            pe(lambda: PE.matmul(mm[pi][:], lhsT=wbuf[b][:, kt, cg * 128:(cg + 1) * 128], rhs=hT[:, kt, :],
                                 start=(kt == 0), stop=(kt == 15)),
               r=[f"w{b}"] + [f"hT{kt}_{s}" for s in range(NS)], w=[f"mm{pi}"], inc=(kt == 15))
        return pi

    def tm_group(b, s, src, srckeys):
        pi = nxt("mm", NMM)
        for kt in range(16):
            pe(lambda: PE.matmul(mm[pi][:], lhsT=src(kt), rhs=wbuf[b][:, kt, :], start=(kt == 0), stop=(kt == 15)),
               r=[f"w{b}", srckeys(kt)], w=[f"mm{pi}"], inc=(kt == 15))
        return pi

    ssl = lambda s: slice(s * 128, (s + 1) * 128)

    def xkeys(t, s):
        return [f"xd{t}_{s}_{c}" for c in range(4)]

    def layer_setup(l):
        hi_rep = k_tm[:, 0:2, :].rearrange("p a (b m) -> p (a b) m", m=128)
        lo_rep = k_tm[:, 2:4, :].rearrange("p a (b m) -> p (a b) m", m=128)
        dve(lambda: V.tensor_copy(out=hi_rep, in_=csil_hi[:, :].unsqueeze(2).to_broadcast([128, 16, 128])), r=["csil_hi"], w=["ktm0", "ktm1"])
        dve(lambda: V.tensor_copy(out=lo_rep, in_=csil_lo[:, :].unsqueeze(2).to_broadcast([128, 16, 128])), r=["csil_lo"], w=["ktm2", "ktm3"])
        brow = xin[0]
        scl, kscl = sclb, "sclb"
        for part in range(3):
            tr.dma("sp", brow[:], adab_in[l:l + 1, part * D:(part + 1) * D].partition_broadcast(128), writes=["xin0"])
            for cc in range(4):
                c0 = part * D + cc * 512
                csl = slice(cc * 512, (cc + 1) * 512)
                pi = nxt("mm", NMM)
                b = nxt("w", NB)
                tr.dma("pool", wbuf[b][:], adaw_in[l, :, c0:c0 + 512].rearrange("(kt p) n -> p kt n", p=128), writes=[f"w{b}"])
                for kt in range(16):
                    pe(lambda: PE.matmul(mm[pi][:], lhsT=hi_rep[:, kt, :], rhs=wbuf[b][:, kt, :], start=(kt == 0), stop=False),
                       r=["ktm0", "ktm1", f"w{b}"], w=[f"mm{pi}"], inc=False)
                    pe(lambda: PE.matmul(mm[pi][:], lhsT=lo_rep[:, kt, :], rhs=wbuf[b][:, kt, :], start=False, stop=(kt == 15)),
                       r=["ktm2", "ktm3", f"w{b}"], w=[f"mm{pi}"], inc=(kt == 15))
                if part == 2:
                    dve(lambda: V.tensor_tensor(out=gate_bc[:, csl], in0=mm[pi][:], in1=brow[:, csl], op=ALU.add),
                        r=[f"mm{pi}", "xin0"], w=["gate_bc"])
                else:
                    tt_, ktt = gettmp()
                    dst, kdst = (shiftc, "shiftc") if part == 0 else (scl, kscl)
                    dve(lambda: V.tensor_tensor(out=tt_[:], in0=mm[pi][:], in1=brow[:, csl], op=ALU.add), r=[f"mm{pi}", "xin0"], w=[ktt])
                    dve(lambda: V.tensor_tensor(out=tt_[:].rearrange("p (a m) -> p a m", m=128), in0=tt_[:].rearrange("p (a m) -> p a m", m=128),
                                                in1=identf[:, :].unsqueeze(1).to_broadcast([128, 4, 128]), op=ALU.mult),
                        r=[ktt, "identf"], w=[ktt])
                    dve(lambda: V.reduce_sum(out=dst[:, cc * 4:(cc + 1) * 4], in_=tt_[:].rearrange("p (a m) -> p a m", m=128), axis=AX.X),
                        r=[ktt], w=[kdst])
        dve(lambda: V.scalar_tensor_tensor(out=gs[:], in0=scl[:, 0:16], scalar=1.0, in1=normg[:, l, :], op0=ALU.add, op1=ALU.mult),
            r=[kscl, "normg"], w=["gs"])
        for h in range(NH):
            dve(lambda: V.memset(S[:, 2 * h:2 * h + 2, :], 0.0), w=[f"S{h}"])
            pool(lambda: P.memset(Sbf[:, 2 * h:2 * h + 2, :], 0.0), w=[f"Sbf{h}"])
        for ct in range(4):
            dve(lambda: V.memset(u_ext[:, ct, 0:2], 0.0), w=[f"u{ct}"])
            dve(lambda: V.memset(glu_ext[:, ct, 0:30], 0.0), w=[f"glu{ct}"])

    def norm_pre(l, t, s):
        xsrc = x_in if l == 0 else x_s
        if s == 0:
            tr.dma("sp", cos_t[:], cos_d[:, t * TT:(t + 1) * TT], reads=[f"cosd{t}"], writes=["cos_t"])
            tr.dma("sp", sin_t[:], sin_d[:, t * TT:(t + 1) * TT], reads=[f"sind{t}"], writes=["sin_t"])
        xb, xk = xin[0], "xin0"
        r0 = t * TT + s * 128
        tr.dma("sp", xb[:], xsrc[r0:r0 + 128, :], reads=(xkeys(t, s) if l > 0 else []), writes=[xk])
        dve(lambda: V.memset(ss[:, 0:1], 0.0), w=["ss"])
        act(lambda: A.activation(out=hb[:], in_=xb[:], func=AF.Square, accum_out=ss[:, 0:1]), r=[xk, "ss"], w=["hb", "ss"])
        act(lambda: A.activation(out=ss[:, 1:2], in_=ss[:, 0:1], func=AF.Sqrt, scale=1.0 / D, bias=epsb[:, 0:1]), r=["ss", "epsb"], w=["ss"])
        dve(lambda: V.reciprocal(out=ss[:, 2:3], in_=ss[:, 1:2]), r=["ss"], w=["ss"])
        act(lambda: A.activation(out=hb[:], in_=xb[:], func=AF.Copy, scale=ss[:, 2:3]), r=[xk, "ss"], w=["hb"])

    def norm_tr(s):
        for g in range(4):
            ti = nxt("tp", 2)
            for j in range(4):
                kt = g * 4 + j
                pe(lambda: PE.transpose(out=tp[ti][:, j * 128:(j + 1) * 128], in_=hb[:, kt * 128:(kt + 1) * 128], identity=ident[:]),
                   r=["hb", "ident"], w=[f"tp{ti}"], inc=(j == 3))
            for j in range(4):
                kt = g * 4 + j
                dve(lambda: V.tensor_scalar(out=hT[:, kt, ssl(s)], in0=tp[ti][:, j * 128:(j + 1) * 128],
                                            scalar1=gs[:, kt:kt + 1], scalar2=shiftc[:, kt:kt + 1], op0=ALU.mult, op1=ALU.add),
                    r=[f"tp{ti}", "gs", "shiftc"], w=[f"hT{kt}_{s}"])

    def item_norm(l, t):
        for s in range(NS):
            norm_pre(l, t, s)
            norm_tr(s)

    def rotary(p1, p2, h, is_q):
        ra, kra = gettmp()
        rb, krb = gettmp()
        rc, krc = gettmp()
        rd, krd = gettmp()
        dve(lambda: V.tensor_tensor(out=ra[:], in0=mm[p1][:], in1=cos_t[:], op=ALU.mult), r=[f"mm{p1}", "cos_t"], w=[kra])
        dve(lambda: V.tensor_tensor(out=rb[:], in0=mm[p2][:], in1=sin_t[:], op=ALU.mult), r=[f"mm{p2}", "sin_t"], w=[krb])
        dve(lambda: V.tensor_tensor(out=rc[:], in0=mm[p2][:], in1=cos_t[:], op=ALU.mult), r=[f"mm{p2}", "cos_t"], w=[krc])
        dve(lambda: V.tensor_tensor(out=rd[:], in0=mm[p1][:], in1=sin_t[:], op=ALU.mult), r=[f"mm{p1}", "sin_t"], w=[krd])
        _stage(4.5)
        if is_q:
            qd = qdec[:, h:h + 1, :].to_broadcast([128, NS, 128])
            pool(lambda: P.tensor_tensor(out=ra[:], in0=ra[:], in1=rb[:], op=ALU.subtract), r=[kra, krb], w=[kra])
            pool(lambda: P.tensor_tensor(out=rc[:], in0=rc[:], in1=rd[:], op=ALU.add), r=[krc, krd], w=[krc])
            _stage(4.7)
            pool(lambda: P.tensor_tensor(out=qT[:, 2 * h, :].rearrange("p (s i) -> p s i", s=NS),
                                         in0=ra[:].rearrange("p (s i) -> p s i", s=NS), in1=qd, op=ALU.mult),
                 r=[kra, "cst"], w=[f"qT{2 * h}"])
            pool(lambda: P.tensor_tensor(out=qT[:, 2 * h + 1, :].rearrange("p (s i) -> p s i", s=NS),
                                         in0=rc[:].rearrange("p (s i) -> p s i", s=NS), in1=qd, op=ALU.mult),
                 r=[krc, "cst"], w=[f"qT{2 * h + 1}"])
        else:
            pool(lambda: P.tensor_tensor(out=kT[:, 2 * h, :], in0=ra[:], in1=rb[:], op=ALU.subtract), r=[kra, krb], w=[f"kT{2 * h}"])
            pool(lambda: P.tensor_tensor(out=kT[:, 2 * h + 1, :], in0=rc[:], in1=rd[:], op=ALU.add), r=[krc, krd], w=[f"kT{2 * h + 1}"])

    def item_qk(is_q, ci):
        b = w_next()
        for hh in range(2):
            h = 2 * ci + hh
            p1 = fm_group(b, 2 * hh)
            p2 = fm_group(b, 2 * hh + 1)
            _stage(4.2)
            rotary(p1, p2, h, is_q)

    def item_ktrans():
        for s in range(NS):
            ti = nxt("tp", 2)
            for j in range(8):
                pe(lambda: PE.transpose(out=tp[ti][:, j * 128:(j + 1) * 128], in_=kT[:, j, ssl(s)], identity=ident[:]),
                   r=[f"kT{j}", "ident"], w=[f"tp{ti}"], inc=(j == 7))
            dve(lambda: V.tensor_tensor(out=k_tm[:, s, :].rearrange("p (h d) -> p h d", h=NH),
                                        in0=tp[ti][:].rearrange("p (h d) -> p h d", h=NH),
                                        in1=kdec.unsqueeze(2).to_broadcast([128, NH, HD]), op=ALU.mult),
                r=[f"tp{ti}", "cst"], w=[f"ktm{s}"])

    def item_vg(is_v, ci):
        b = w_next()
        for s in range(NS):
            pi = tm_group(b, s, lambda kt: hT[:, kt, ssl(s)], lambda kt: f"hT{kt}_{s}")
            if is_v:
                act(lambda: A.activation(out=v_tm[:, s, ci * 512:(ci + 1) * 512], in_=mm[pi][:], func=AF.Copy),
                    r=[f"mm{pi}"], w=[f"v{s}_{ci}"])
            else:
                act(lambda: A.activation(out=sg[:, s, ci * 512:(ci + 1) * 512], in_=mm[pi][:], func=AF.Silu),
                    r=[f"mm{pi}"], w=[f"sg{s}_{ci}"])

    def item_cf(l, kind):
        b = w_next()
        for ct in range(4):
            pi = fm_group(b, ct)
            if kind == "fb":
                act(lambda: A.activation(out=cfy[:, ct, :], in_=mm[pi][:], func=AF.Sigmoid), r=[f"mm{pi}"], w=[f"cfy{ct}"])
            elif kind == "fa":
                dve(lambda: V.tensor_tensor(out=glu_ext[:, ct, 30:30 + TT], in0=mm[pi][:], in1=cfy[:, ct, :], op=ALU.mult),
                    r=[f"mm{pi}", f"cfy{ct}"], w=[f"glu{ct}"])

                def tap0(ct=ct):
                    dve(lambda: V.tensor_scalar(out=cfy[:, ct, :], in0=glu_ext[:, ct, 0:TT], scalar1=cfw[:, l, ct, 0:1],
                                                scalar2=cfp[:, l, 0, ct:ct + 1], op0=ALU.mult, op1=ALU.add),
                        r=[f"glu{ct}", "cfw", "cfp"], w=[f"cfy{ct}"])
                bg.append(tap0)
                for k in range(1, CFW):
                    def tapk(ct=ct, k=k):
                        dve(lambda: V.scalar_tensor_tensor(out=cfy[:, ct, :], in0=glu_ext[:, ct, k:k + TT], scalar=cfw[:, l, ct, k:k + 1],
                                                           in1=cfy[:, ct, :], op0=ALU.mult, op1=ALU.add),
                            r=[f"glu{ct}", "cfw", f"cfy{ct}"], w=[f"cfy{ct}"])
                    bg.append(tapk)

                def carry(ct=ct):
                    pool(lambda: P.tensor_copy(out=glu_ext[:, ct, 0:30], in_=glu_ext[:, ct, TT:TT + 30]), r=[f"glu{ct}"], w=[f"glu{ct}"])
                bg.append(carry)
            else:
                sgt, ksg = gettmp()
                act(lambda: A.activation(out=sgt[:], in_=mm[pi][:], func=AF.Silu), r=[f"mm{pi}"], w=[ksg])
                pool(lambda: P.tensor_tensor(out=mixedT[:, 12 + ct, :], in0=cfy[:, ct, :], in1=sgt[:], op=ALU.mult),
                     r=[f"cfy{ct}", ksg], w=[f"mx{12 + ct}"])
    def item_cf_ln(l):
        bg_pump(len(bg))
        if True:
            pm = nxt("mm", NMM)
            pq = nxt("mm", NMM)
            for ct in range(4):
                t1, k1 = gettmp()
                t2, k2 = gettmp()
                hi = t1[:].bitcast(BF16)[:, 0:TT]
                lo = t1[:].bitcast(BF16)[:, TT:2 * TT]
                sq = t2[:].bitcast(BF16)[:, 0:TT]
                act(lambda: A.activation(out=hi, in_=cfy[:, ct, :], func=AF.Copy), r=[f"cfy{ct}"], w=[k1])
                pool(lambda: P.tensor_tensor(out=lo, in0=cfy[:, ct, :], in1=hi, op=ALU.subtract), r=[f"cfy{ct}", k1], w=[k1])
                act(lambda: A.activation(out=sq, in_=cfy[:, ct, :], func=AF.Square), r=[f"cfy{ct}"], w=[k2])
                pe(lambda: PE.matmul(mm[pm][:], lhsT=ones_b[:], rhs=hi, start=(ct == 0), stop=False),
                   r=["ones_b", k1], w=[f"mm{pm}"], inc=False)
                pe(lambda: PE.matmul(mm[pm][:], lhsT=ones_b[:], rhs=lo, start=False, stop=(ct == 3)),
                   r=["ones_b", k1], w=[f"mm{pm}"], inc=(ct == 3))
                pe(lambda: PE.matmul(mm[pq][:], lhsT=ones_b[:], rhs=sq, start=(ct == 0), stop=(ct == 3)),
                   r=["ones_b", k2], w=[f"mm{pq}"], inc=(ct == 3))
            act(lambda: A.activation(out=mean_sb[:], in_=mm[pm][:], func=AF.Copy), r=[f"mm{pm}"], w=["mean_sb"])
            pool(lambda: P.tensor_tensor(out=rstd_sb[:], in0=mean_sb[:], in1=mean_sb[:], op=ALU.mult), r=["mean_sb"], w=["rstd_sb"])
            dve(lambda: V.tensor_tensor(out=rstd_sb[:], in0=mm[pq][:], in1=rstd_sb[:], op=ALU.subtract), r=[f"mm{pq}", "rstd_sb"], w=["rstd_sb"])
            act(lambda: A.activation(out=rstd_sb[:], in_=rstd_sb[:], func=AF.Sqrt, bias=epsb[:, 0:1]), r=["rstd_sb", "epsb"], w=["rstd_sb"])
            dve(lambda: V.reciprocal(out=rstd_sb[:], in_=rstd_sb[:]), r=["rstd_sb"], w=["rstd_sb"])
            for ct in range(4):
                pool(lambda: P.tensor_tensor(out=cfy[:, ct, :], in0=cfy[:, ct, :], in1=mean_sb[:], op=ALU.subtract),
                     r=[f"cfy{ct}", "mean_sb"], w=[f"cfy{ct}"])
                pool(lambda: P.tensor_tensor(out=cfy[:, ct, :], in0=cfy[:, ct, :], in1=rstd_sb[:], op=ALU.mult),
                     r=[f"cfy{ct}", "rstd_sb"], w=[f"cfy{ct}"])
                act(lambda: A.activation(out=cfy[:, ct, :], in_=cfy[:, ct, :], func=AF.Silu, scale=cfp[:, l, 1, ct:ct + 1],
                                         bias=cfp[:, l, 2, ct:ct + 1]), r=[f"cfy{ct}", "cfp"], w=[f"cfy{ct}"])

    def item_sc(l, kind):
        b = w_next()
        for ct in range(4):
            pi = fm_group(b, ct)
            if kind == "sc":
                act(lambda: A.activation(out=scy[:, ct, :], in_=mm[pi][:], func=AF.Copy), r=[f"mm{pi}"], w=[f"scy{ct}"])
            elif kind == "sh":
                dve(lambda: V.tensor_tensor(out=u_ext[:, ct, 2:2 + TT], in0=mm[pi][:], in1=scy[:, ct, :], op=ALU.mult),
                    r=[f"mm{pi}", f"scy{ct}"], w=[f"u{ct}"])
                dve(lambda: V.tensor_scalar(out=scy[:, ct, :], in0=u_ext[:, ct, 0:TT], scalar1=scw[:, l, ct, 0:1], scalar2=None, op0=ALU.mult),
                    r=[f"u{ct}", "scw"], w=[f"scy{ct}"])
                for k in range(1, SCW):
                    dve(lambda: V.scalar_tensor_tensor(out=scy[:, ct, :], in0=u_ext[:, ct, k:k + TT], scalar=scw[:, l, ct, k:k + 1],
                                                       in1=scy[:, ct, :], op0=ALU.mult, op1=ALU.add),
                        r=[f"u{ct}", "scw", f"scy{ct}"], w=[f"scy{ct}"])
                pool(lambda: P.tensor_copy(out=u_ext[:, ct, 0:2], in_=u_ext[:, ct, TT:TT + 2]), r=[f"u{ct}"], w=[f"u{ct}"])
                bg_pump(3)
            elif kind == "sg":
                sgt, ksg = gettmp()
                act(lambda: A.activation(out=sgt[:], in_=mm[pi][:], func=AF.Silu), r=[f"mm{pi}"], w=[ksg])
                pool(lambda: P.tensor_tensor(out=scy[:, ct, :], in0=scy[:, ct, :], in1=sgt[:], op=ALU.mult), r=[f"scy{ct}", ksg], w=[f"scy{ct}"])
            else:
                dve(lambda: V.tensor_tensor(out=mixedT[:, 8 + ct, :], in0=mm[pi][:], in1=scy[:, ct, :], op=ALU.mult),
                    r=[f"mm{pi}", f"scy{ct}"], w=[f"mx{8 + ct}"])
                bg_pump(3)

    def item_ret():
        for s in range(NS):
            for h in range(NH):
                for half in range(2):
                    pe(lambda: PE.matmul(sc_ps[:, h * 128:(h + 1) * 128], lhsT=kT[:, 2 * h + half, ssl(s)], rhs=qT[:, 2 * h + half, ssl(s)],
                                         start=(half == 0), stop=(half == 1)),
                       r=[f"kT{2 * h + half}", f"qT{2 * h + half}"], w=["sc_ps"], inc=(h == NH - 1 and half == 1))
            dve(lambda: V.tensor_tensor(out=sT[:], in0=sc_ps[:].rearrange("p (h i) -> p h i", h=NH), in1=maskT, op=ALU.mult),
                r=["sc_ps", "cst"], w=["sT"])
            bg_pump(2)
            for hp in range(2):
                for hh in range(2):
                    h = 2 * hp + hh
                    osl = slice(0, 256)
                    o_ps = o_pss[hh]
                    vsl = slice(h * 256, (h + 1) * 256)
                    pe(lambda: PE.matmul(o_ps[:, osl], lhsT=sT[:, h, :], rhs=v_tm[:, s, vsl], start=True, stop=False),
                       r=["sT", f"v{s}_{h // 2}"], w=[f"o{hh}"], inc=False)
                    pe(lambda: PE.matmul(o_ps[:, osl], lhsT=qT[:, 2 * h, ssl(s)], rhs=Sbf[:, 2 * h, :], start=False, stop=False),
                       r=[f"qT{2 * h}", f"Sbf{h}"], w=[f"o{hh}"], inc=False)
                    pe(lambda: PE.matmul(o_ps[:, osl], lhsT=qT[:, 2 * h + 1, ssl(s)], rhs=Sbf[:, 2 * h + 1, :], start=False, stop=True),
                       r=[f"qT{2 * h + 1}", f"Sbf{h}"], w=[f"o{hh}"])
                    dve(lambda: V.bn_stats(out=st6[:, h, :], in_=o_ps[:, osl]), r=[f"o{hh}"], w=[f"st{h}"])
                    dve(lambda: V.bn_aggr(out=mv[:, h, :], in_=st6[:, h, :]), r=[f"st{h}"], w=[f"mv{h}"])
                    act(lambda: A.activation(out=gn[:, h, 0:1], in_=mv[:, h, 1:2], func=AF.Sqrt, bias=epsb[:, 0:1]), r=[f"mv{h}", "epsb"], w=[f"gn{h}"])
                    dve(lambda: V.reciprocal(out=gn[:, h, 1:2], in_=gn[:, h, 0:1]), r=[f"gn{h}"], w=[f"gn{h}"])
                    dve(lambda: V.scalar_tensor_tensor(out=gn[:, h, 2:3], in0=mv[:, h, 0:1], scalar=-1.0, in1=gn[:, h, 1:2],
                                                       op0=ALU.mult, op1=ALU.mult), r=[f"mv{h}", f"gn{h}"], w=[f"gn{h}"])
                    dve(lambda: V.tensor_scalar(out=on[hh][:], in0=o_ps[:, osl], scalar1=gn[:, h, 1:2], scalar2=gn[:, h, 2:3], op0=ALU.mult, op1=ALU.add),
                        r=[f"o{hh}", f"gn{h}"], w=[f"on{hh}"])
                    pool(lambda: P.tensor_tensor(out=og[:, vsl], in0=on[hh][:], in1=sg[:, s, vsl], op=ALU.mult),
                         r=[f"on{hh}", f"sg{s}_{h // 2}"], w=[f"og{h}"])
                    bg_pump(3)
            for h in range(NH):
                pi = nxt("mm", NMM)
                vsl = slice(h * 256, (h + 1) * 256)
                for half in range(2):
                    pe(lambda: PE.matmul(mm[pi][:, half * 256:(half + 1) * 256], lhsT=k_tm[:, s, h * 256 + half * 128:h * 256 + (half + 1) * 128],
                                         rhs=v_tm[:, s, vsl], start=True, stop=True),
                       r=[f"ktm{s}", f"v{s}_{h // 2}"], w=[f"mm{pi}"], inc=(half == 1))
                Sv = S[:, 2 * h:2 * h + 2, :].rearrange("p a b -> p (a b)")
                dve(lambda: V.scalar_tensor_tensor(out=Sv, in0=Sv, scalar=CDEC[h], in1=mm[pi][:], op0=ALU.mult, op1=ALU.add),
                    r=[f"S{h}", f"mm{pi}"], w=[f"S{h}"])
                pool(lambda: P.tensor_copy(out=Sbf[:, 2 * h:2 * h + 2, :], in_=S[:, 2 * h:2 * h + 2, :]), r=[f"S{h}"], w=[f"Sbf{h}"])
                bg_pump(3)
            ti = nxt("tp", 2)
            for j in range(8):
                pe(lambda: PE.transpose(out=tp[ti][:, j * 128:(j + 1) * 128], in_=og[:, j * 128:(j + 1) * 128], identity=ident[:]),
                   r=[f"og{j // 2}", "ident"], w=[f"tp{ti}"], inc=(j == 7))
            dve(lambda: V.tensor_copy(out=mixedT[:, 0:8, ssl(s)], in_=tp[ti][:].rearrange("p (j t) -> p j t", j=8)),
                r=[f"tp{ti}"], w=[f"mxr{s}"])

    def item_out(l, t, nxt_norm=None):
        if nxt_norm is not None:
            norm_pre(l, nxt_norm, 0)
        for cc in range(4):
            if nxt_norm is not None and cc > 0:
                norm_tr(cc - 1)
                norm_pre(l, nxt_norm, cc)
            b = w_next()
            csl = slice(cc * 512, (cc + 1) * 512)
            for s in range(NS):
                pi = tm_group(b, s, lambda kt: mixedT[:, kt, ssl(s)], lambda kt: (f"mxr{s}" if kt < 8 else f"mx{kt}"))
                r0 = t * TT + s * 128
                xp, kxp = gettmp()
                gt, kgt = gettmp()
                xsrc = x_in if l == 0 else x_s
                tr.dma("sp", xp[:], xsrc[r0:r0 + 128, csl], reads=([f"xd{t}_{s}_{cc}"] if l > 0 else []), writes=[kxp])
                dve(lambda: V.tensor_tensor(out=gt[:], in0=mm[pi][:], in1=gate_bc[:, csl], op=ALU.mult), r=[f"mm{pi}", "gate_bc"], w=[kgt])
                pool(lambda: P.tensor_tensor(out=xp[:], in0=xp[:], in1=gt[:], op=ALU.add), r=[kxp, kgt], w=[kxp])
                tr.dma("sp", x_s[r0:r0 + 128, csl], xp[:], reads=[kxp], writes=[f"xd{t}_{s}_{cc}"])
        if nxt_norm is not None:
            norm_tr(3)

    for l in range(L):
        layer_setup(l)
        for t in range(NT):
            if t == 0:
                item_norm(l, t)
            item_qk(True, 0)
            item_qk(True, 1)
            item_qk(False, 0)
            item_qk(False, 1)
            item_ktrans()
            item_vg(True, 0)
            item_vg(True, 1)
            item_vg(False, 0)
            item_vg(False, 1)
            item_cf(l, "fb")
            item_cf(l, "fa")
            item_sc(l, "sc")
            item_sc(l, "sh")
            item_sc(l, "sg")
            item_sc(l, "sb")
            item_ret()
            item_cf_ln(l)
            item_cf(l, "fg")
            item_out(l, t, nxt_norm=(t + 1 if t + 1 < NT else None))

    tr.dma("sp", gate_bc[:], fg_in.partition_broadcast(128), writes=["gate_bc"])
    for t in range(NT):
        for s in range(NS):
            xb, xk = xin[0], "xin0"
            r0 = t * TT + s * 128
            tr.dma("sp", xb[:], x_s[r0:r0 + 128, :], reads=xkeys(t, s), writes=[xk])
            dve(lambda: V.memset(ss[:, 0:1], 0.0), w=["ss"])
            act(lambda: A.activation(out=hb[:], in_=xb[:], func=AF.Square, accum_out=ss[:, 0:1]), r=[xk, "ss"], w=["hb", "ss"])
            act(lambda: A.activation(out=ss[:, 1:2], in_=ss[:, 0:1], func=AF.Sqrt, scale=1.0 / D, bias=epsb[:, 0:1]), r=["ss", "epsb"], w=["ss"])
            dve(lambda: V.reciprocal(out=ss[:, 2:3], in_=ss[:, 1:2]), r=["ss"], w=["ss"])
            dve(lambda: V.scalar_tensor_tensor(out=xb[:], in0=xb[:], scalar=ss[:, 2:3], in1=gate_bc[:], op0=ALU.mult, op1=ALU.mult),
                r=[xk, "ss", "gate_bc"], w=[xk])
            tr.dma("sp", out[r0:r0 + 128, :], xb[:], reads=[xk], writes=[f"out{t}_{s}"])
    tr.wait_all("sp")


def host_consts():
    i = np.arange(128, dtype=np.float64)
    maskT = np.zeros((128, NH, 128), np.float64)
    qdec = np.zeros((128, NH, 128), np.float64)
    kdec = np.zeros((128, NH), np.float64)
    for h in range(NH):
        g = GAM[h]
        maskT[:, h, :] = (i[None, :] >= i[:, None]) * (g ** (-(i[:, None] + 1.0))) / 16.0
        qdec[:, h, :] = (g ** (i + 1.0))[None, :]
        kdec[:, h] = g ** (127.0 - i) / 16.0
    invf = 10000.0 ** (-np.arange(128, dtype=np.float32) / np.float32(128))
    cst = np.concatenate([maskT.reshape(128, -1), qdec.reshape(128, -1), kdec, invf.astype(np.float64)[:, None]], axis=1)
    return np.ascontiguousarray(cst.astype(np.float32))


def make_in_maps(x, c, positions, ada_w, ada_b, norm_g, w_in, sc_conv_w, cf_conv_w, cf_conv_b, cf_ln_g, cf_ln_b, w_out, final_g):
    B, T, _ = x.shape
    L = w_in.shape[0]
    f = lambda a: np.ascontiguousarray(np.asarray(a, dtype=np.float32))
    cst = host_consts()
    normg = f(np.asarray(norm_g).reshape(L, 16, 128).transpose(2, 0, 1))
    scw = f(np.asarray(sc_conv_w).reshape(L, SCW, 4, 128).transpose(3, 0, 2, 1))
    cfw = f(np.asarray(cf_conv_w).reshape(L, CFW, 4, 128).transpose(3, 0, 2, 1))
    cfp = f(np.stack([np.asarray(cf_conv_b), np.asarray(cf_ln_g), np.asarray(cf_ln_b)], axis=1).reshape(L, 3, 4, 128).transpose(3, 0, 1, 2))
    shared = {"ada_w": f(ada_w), "ada_b": f(ada_b), "normg": normg, "w_in": f(w_in), "w_out": f(w_out), "scw": scw,
              "cfw": cfw, "cfp": cfp, "final_g": f(final_g).reshape(1, D), "cst": cst}
    maps = []
    for b in range(B):
        m = dict(shared)
        m["x"] = f(x[b])
        m["pos"] = np.ascontiguousarray(np.asarray(positions[b], dtype=np.int32).reshape(1, T))
        m["ccol"] = f(np.asarray(c[b]).reshape(16, 128).T)
        maps.append(m)
    return maps


_NC_CACHE = {}


def kernel(x, c, positions, ada_w, ada_b, norm_g, w_in, sc_conv_w, cf_conv_w, cf_conv_b, cf_ln_g, cf_ln_b, w_out, final_g):
    x = np.asarray(x)
    B, T, _ = x.shape
    L = np.asarray(w_in).shape[0]
    maps = make_in_maps(x, c, positions, ada_w, ada_b, norm_g, w_in, sc_conv_w, cf_conv_w, cf_conv_b, cf_ln_g, cf_ln_b, w_out, final_g)
    key = (T, L)
    if key not in _NC_CACHE:
        _NC_CACHE[key] = build_nc(T, L)
    nc = _NC_CACHE[key]
    res = run_bass_kernel_spmd(nc, maps, core_ids=list(range(B)))
    return np.stack([np.asarray(res.results[b]["out"]) for b in range(B)], axis=0).astype(np.float32)
```
